# Optimizing a Trainium2 kernel written in Bass

```python
import math
import jax, jax.numpy as jnp
from jax import lax
import numpy as np

D_MODEL = 1024
BATCH = 4
SEQ = 8192
DEPTH = 1

D_MIX = D_MODEL
D_ATTN = D_MIX // 2
D_SGU = D_MIX - D_ATTN
ATTN_HEADS = 4
ATTN_HEAD_DIM = D_ATTN // (2 * ATTN_HEADS)
ATTN_V_DIM = 2 * ATTN_HEAD_DIM
SGU_GROUPS = 4
SGU_GROUP_DIM = D_SGU // SGU_GROUPS
CHUNK = 128
Q_BLOCK = 128
D_IN = 3 * D_ATTN + 2 * D_SGU
D_FF = -(-8 * D_MODEL // (3 * 256)) * 256
DEEPNORM_ALPHA = (2 * DEPTH) ** 0.25
DEEPNORM_BETA = (8 * DEPTH) ** -0.25
LN_EPS = 1e-5
RMS_EPS = 1e-5

kernel_name = "hymba_diffattn_sgu_deepnorm_adaln"


def layer_norm(x, g, b):
    xf = x.astype(jnp.float32)
    mu = jnp.mean(xf, axis=-1, keepdims=True)
    var = jnp.mean(jnp.square(xf - mu), axis=-1, keepdims=True)
    y = (xf - mu) * lax.rsqrt(var + LN_EPS)
    return (y * g.astype(jnp.float32) + b.astype(jnp.float32)).astype(x.dtype)


def rms_norm(x, g):
    xf = x.astype(jnp.float32)
    y = xf * lax.rsqrt(jnp.mean(jnp.square(xf), axis=-1, keepdims=True) + RMS_EPS)
    return (y * g.astype(jnp.float32)).astype(x.dtype)


def modulate(x, shift, scale):
    return x * (1.0 + scale[:, None, :]) + shift[:, None, :]


def diff_attention(q, k, v, lam):
    b, s, h, _, dh = q.shape
    e = v.shape[-1]
    nb = s // Q_BLOCK
    qb = q.reshape(b, nb, Q_BLOCK, h, 2, dh).transpose(1, 0, 2, 3, 4, 5)
    key_pos = jnp.arange(s)
    scale = dh ** -0.5

    def one_block(args):
        q_blk, blk = args
        sc = jnp.einsum('bqhmd,bkhmd->bhmqk', q_blk, k,
                        preferred_element_type=jnp.float32) * scale
        q_pos = blk * Q_BLOCK + jnp.arange(Q_BLOCK)
        mask = key_pos[None, :] <= q_pos[:, None]
        sc = jnp.where(mask, sc, -jnp.inf)
        p = jax.nn.softmax(sc, axis=-1)
        w = p[:, :, 0] - lam * p[:, :, 1]
        return jnp.einsum('bhqk,bkhe->bqhe', w.astype(v.dtype), v)

    out = lax.map(one_block, (qb, jnp.arange(nb)))
    return out.transpose(1, 0, 2, 3, 4).reshape(b, s, h, e)


def spatial_gating(u, v, ln_g, ln_b, w_s, b_s):
    b, s, g, ch = v.shape
    v = layer_norm(v, ln_g.reshape(g, ch), ln_b.reshape(g, ch))
    vc = v.reshape(b, s // CHUNK, CHUNK, g, ch)
    causal = jnp.tril(jnp.ones((CHUNK, CHUNK), dtype=w_s.dtype))
    w = w_s * causal[None]
    gate = jnp.einsum('gts,bnsgc->bntgc', w, vc) + b_s.T[None, None, :, :, None]
    return u * gate.reshape(b, s, g, ch)


def setup_inputs(seed: int = 0) -> dict:
    key = jax.random.key(seed)
    ks = jax.random.split(key, 24)
    L, D = DEPTH, D_MODEL
    nrm = lambda k, shape, s: jax.random.normal(k, shape, jnp.float32) * s
    return {
        "x": nrm(ks[0], (BATCH, SEQ, D), 1.0),
        "c": nrm(ks[1], (BATCH, D), 1.0),
        "w_ada": nrm(ks[2], (L, D, 6 * D), 0.1 * D ** -0.5),
        "b_ada": nrm(ks[3], (L, 6 * D), 0.01),
        "w_in": nrm(ks[4], (L, D, D_IN), D ** -0.5),
        "lambda_q1": nrm(ks[5], (L, ATTN_HEAD_DIM), 0.1),
        "lambda_k1": nrm(ks[6], (L, ATTN_HEAD_DIM), 0.1),
        "lambda_q2": nrm(ks[7], (L, ATTN_HEAD_DIM), 0.1),
        "lambda_k2": nrm(ks[8], (L, ATTN_HEAD_DIM), 0.1),
        "subln_g": 1.0 + nrm(ks[9], (L, ATTN_V_DIM), 0.02),
        "sgu_ln_g": 1.0 + nrm(ks[10], (L, D_SGU), 0.02),
        "sgu_ln_b": nrm(ks[11], (L, D_SGU), 0.02),
        "w_spatial": nrm(ks[12], (L, SGU_GROUPS, CHUNK, CHUNK), 0.5 * CHUNK ** -0.5),
        "b_spatial": 1.0 + nrm(ks[13], (L, SGU_GROUPS, CHUNK), 0.1),
        "w_out": nrm(ks[14], (L, D_MIX, D), DEEPNORM_BETA * D_MIX ** -0.5),
        "ln1_g": 1.0 + nrm(ks[15], (L, D), 0.02),
        "ln1_b": nrm(ks[16], (L, D), 0.02),
        "w_gate": nrm(ks[17], (L, D, D_FF), D ** -0.5),
        "w_up": nrm(ks[18], (L, D, D_FF), D ** -0.5),
        "w_down": nrm(ks[19], (L, D_FF, D), DEEPNORM_BETA * D_FF ** -0.5),
        "ln2_g": 1.0 + nrm(ks[20], (L, D), 0.02),
        "ln2_b": nrm(ks[21], (L, D), 0.02),
    }


def reference(x, c, w_ada, b_ada, w_in, lambda_q1, lambda_k1, lambda_q2, lambda_k2,
              subln_g, sgu_ln_g, sgu_ln_b, w_spatial, b_spatial, w_out,
              ln1_g, ln1_b, w_gate, w_up, w_down, ln2_g, ln2_b):
    b, s, _ = x.shape
    c_act = jax.nn.silu(c)
    for l in range(DEPTH):
        mod = c_act @ w_ada[l] + b_ada[l]
        sh1, sc1, g1, sh2, sc2, g2 = jnp.split(mod, 6, axis=-1)

        h = modulate(x, sh1, sc1)
        proj = h @ w_in[l]
        q, k, v_a, z = jnp.split(proj, [D_ATTN, 2 * D_ATTN, 3 * D_ATTN], axis=-1)
        q = q.reshape(b, s, ATTN_HEADS, 2, ATTN_HEAD_DIM)
        k = k.reshape(b, s, ATTN_HEADS, 2, ATTN_HEAD_DIM)
        v_a = v_a.reshape(b, s, ATTN_HEADS, ATTN_V_DIM)

        lam_init = 0.8 - 0.6 * math.exp(-0.3 * l)
        lam = (jnp.exp(jnp.sum(lambda_q1[l].astype(jnp.float32) * lambda_k1[l].astype(jnp.float32)))
               - jnp.exp(jnp.sum(lambda_q2[l].astype(jnp.float32) * lambda_k2[l].astype(jnp.float32)))
               + lam_init)
        a = diff_attention(q, k, v_a, lam)
        a = rms_norm(a, subln_g[l]) * (1.0 - lam_init)

        z = jax.nn.gelu(z)
        u, vs = jnp.split(z, 2, axis=-1)
        u = u.reshape(b, s, SGU_GROUPS, SGU_GROUP_DIM)
        vs = vs.reshape(b, s, SGU_GROUPS, SGU_GROUP_DIM)
        g_out = spatial_gating(u, vs, sgu_ln_g[l], sgu_ln_b[l], w_spatial[l], b_spatial[l])

        mixed = jnp.concatenate([a.reshape(b, s, D_ATTN), g_out.reshape(b, s, D_SGU)], axis=-1)
        mixed = mixed @ w_out[l]
        x = layer_norm(DEEPNORM_ALPHA * x + (1.0 + g1)[:, None, :] * mixed, ln1_g[l], ln1_b[l])

        h = modulate(x, sh2, sc2)
        f = (jax.nn.silu(h @ w_gate[l]) * (h @ w_up[l])) @ w_down[l]
        x = layer_norm(DEEPNORM_ALPHA * x + (1.0 + g2)[:, None, :] * f, ln2_g[l], ln2_b[l])
    return x
```

```python
import numpy as np
from contextlib import ExitStack
import concourse.bass as bass
import concourse.mybir as mybir
from concourse.bass_utils import run_bass_kernel_spmd

F32 = mybir.dt.float32
BF16 = mybir.dt.bfloat16
AF = mybir.ActivationFunctionType
ALU = mybir.AluOpType
AX = mybir.AxisListType

D = 1024
SEQ = 8192
NB = 4
DIN = 2560
DFF = 2816
NSLOT = 8
ALPHA = float(2.0 ** 0.25)
LAM_INIT = 0.2
LN_EPS = 1e-5
RMS_EPS = 1e-5
NEG = -30000.0
ENG = ("pe", "act", "dve", "pool", "sp")


class Sched:
    def __init__(self, nc, stack):
        self.nc = nc
        self.stack = stack
        self.ops = {e: [] for e in ENG}
        self.sem = {e: stack.enter_context(nc.semaphore("prog_" + e)) for e in ENG if e != "sp"}
        self.cnt = {e: 0 for e in ENG}
        self.seen = {e: {} for e in ENG}
        self.last_w = {}
        self.readers = {}
        self.dma_sems = {}
        self.self_sync = True

    def _deps(self, eng, reads, writes):
        need = {}

        def add(t):
            s, v, e = t
            if e == eng and (eng == "pe" or not self.self_sync):
                return
            k = id(s)
            if k not in need or need[k][1] < v:
                need[k] = (s, v)

        for b in reads:
            if b in self.last_w:
                add(self.last_w[b])
        for b in writes:
            if b in self.last_w:
                add(self.last_w[b])
            for t in self.readers.get(b, ()):
                add(t)
        out = []
        seen = self.seen[eng]
        for k, (s, v) in need.items():
            if seen.get(k, 0) >= v:
                continue
            seen[k] = v
            out.append((s, v))
        return out

    def _commit(self, tok, reads, writes):
        for b in writes:
            self.last_w[b] = tok
            self.readers[b] = []
        for b in reads:
            self.readers.setdefault(b, []).append(tok)

    def op(self, eng, fn, reads=(), writes=()):
        waits = self._deps(eng, reads, writes)
        self.cnt[eng] += 1
        tok = (self.sem[eng], self.cnt[eng], eng)
        self.ops[eng].append((waits, fn, self.sem[eng], 1))
        self._commit(tok, reads, writes)

    def dma(self, q, key, fn, reads=(), writes=(), n=1):
        if key not in self.dma_sems:
            self.dma_sems[key] = [self.stack.enter_context(self.nc.semaphore("dma_%d" % len(self.dma_sems))), 0]
        ent = self.dma_sems[key]
        waits = self._deps(q, reads, writes)
        ent[1] += 16 * n
        tok = (ent[0], ent[1], "dma")
        self.ops[q].append((waits, fn, ent[0], 16))
        self._commit(tok, reads, writes)

    def barrier(self):
        for e in ENG:
            waits = []
            seen = self.seen[e]
            for o in self.sem:
                if o == e:
                    continue
                s, v = self.sem[o], self.cnt[o]
                if v > seen.get(id(s), 0):
                    seen[id(s)] = v
                    waits.append((s, v))
            for k, (s, v) in self.dma_sems.items():
                if v > seen.get(id(s), 0):
                    seen[id(s)] = v
                    waits.append((s, v))
            if waits:
                self.ops[e].append((waits, None, None, 0))
        self.last_w = {}
        self.readers = {}

    def final_wait(self, eng, keys):
        waits = [(self.dma_sems[k][0], self.dma_sems[k][1]) for k in keys]
        self.ops[eng].append((waits, None, None, 0))

    def emit(self, block):
        engs = {"pe": block.tensor, "act": block.scalar, "dve": block.vector,
                "pool": block.gpsimd, "sp": block.sync}
        for e in ENG:
            ops = self.ops[e]

            def body(engine, ops=ops):
                for waits, fn, sem, inc in ops:
                    for s, v in waits:
                        engine.wait_ge(s, v)
                    if fn is None:
                        continue
                    r = fn(engine)
                    if isinstance(r, (list, tuple)):
                        for i in r:
                            i.then_inc(sem, inc)
                    else:
                        r.then_inc(sem, inc)

            engs[e](body)


def build_nc(debug=None, nslot_a=NSLOT, nslot_b=NSLOT):
    nc = bass.Bass("TRN2", target_bir_lowering=False)
    din = lambda name, shape: nc.dram_tensor(name, shape, F32, kind="ExternalInput").ap()
    xs = din("xs", [SEQ, D])
    ccol = din("ccol", [128, 8])
    pbias_d = din("pbias", [128, 8])
    w_ada = din("w_ada", [D, 6 * D])
    b_ada = din("b_ada", [1, 6 * D])
    w_in = din("w_in", [D, DIN])
    lam4 = din("lam4", [4, 64])
    subln_g = din("subln_g", [1, 128])
    sgu_g = din("sgu_ln_g", [1, 512])
    sgu_b = din("sgu_ln_b", [1, 512])
    w_sp = din("w_spatial", [4, 128, 128])
    b_sp = din("b_spatial", [4, 128])
    w_out = din("w_out", [D, D])
    ln1_g = din("ln1_g", [1, D])
    ln1_b = din("ln1_b", [1, D])
    w_gate = din("w_gate", [D, DFF])
    w_up = din("w_up", [D, DFF])
    w_down = din("w_down", [DFF, D])
    ln2_g = din("ln2_g", [1, D])
    ln2_b = din("ln2_b", [1, D])
    y = nc.dram_tensor("y", [NSLOT * 512, D], F32, kind="ExternalOutput").ap()
    dbg = None
    if debug:
        dbg = nc.dram_tensor("dbg", [128, 8 * 512], F32, kind="ExternalOutput").ap()
    scr = lambda name, shape: nc.dram_tensor(name, shape, BF16, kind="Internal").ap()
    wi_b = scr("wi_b", [D, DIN])
    wo_b = scr("wo_b", [D, D])
    wg_b = scr("wg_b", [D, DFF])
    wu_b = scr("wu_b", [D, DFF])
    wd_b = scr("wd_b", [DFF, D])

    with ExitStack() as st:
        T = lambda name, shape, dt: st.enter_context(nc.sbuf_tensor(name, shape, dt))
        S = Sched(nc, st)
        KT = T("KT", [128, 4, SEQ], BF16)
        arena = T("arena", [128, 33792], BF16)
        Vt = arena[:, 0:33280].rearrange("p (k h e) -> p k h e", k=64, h=4, e=130)
        wraw = [T("wraw%d" % i, [128, 2048], F32) for i in range(2)]
        xt_all = T("xt_all", [128, 4 * D], F32)
        xt = [xt_all[:, i * D:(i + 1) * D] for i in range(4)]
        wux = xt_all[:, 0:2048].bitcast(BF16).rearrange("p (k n) -> p k n", k=8)
        wqx = xt_all[:, 2048:4096].bitcast(BF16).rearrange("p (k n) -> p k n", k=8)
        hT = T("hT", [128, 8, 512], BF16)
        scrA = T("scrA", [128, 6656], F32)
        ident = T("ident", [128, 128], F32)
        identb = T("identb", [128, 128], BF16)
        triB = T("triB", [128, 128], BF16)
        wsT = T("wsT", [128, 4, 128], BF16)
        bsb = T("bsb", [128, 4, 128], F32)
        sgg = T("sgg", [128, 512], F32)
        sgb = T("sgb", [128, 512], F32)
        subg = T("subg", [128, 128], F32)
        ones_t = T("ones_t", [128, 128], F32)
        cols = T("cols", [128, 6, 8], F32)
        pbias = T("pbias_t", [128, 8], F32)
        small = T("small", [128, 64], F32)
        stats_all = T("stats", [128, 4, 24], F32)
        mv_all = T("mv", [128, 4, 8], F32)
        lnsm = T("lnsm", [128, 4, 8], F32)
        pst = st.enter_context(nc.psum_tensor("pst", [128, 8 * 512], F32))
        ps = [pst[:, i * 512:(i + 1) * 512] for i in range(8)]
        psb = [p.bitcast(BF16) for p in ps]
        S2 = [pst[:, 512:1536].rearrange("p (m n) -> p m n", m=2), pst[:, 1536:2560].rearrange("p (m n) -> p m n", m=2)]
        S2K = [["B1", "B2"], ["B3", "B4"]]

        QTall = scrA[:, 0:1024].bitcast(BF16)
        QTall2 = wraw[0][:, 0:1024].bitcast(BF16)
        QTp = [scrA[:, 512 * i:512 * (i + 1)].bitcast(BF16).rearrange("p (m n) -> p m n", m=2) for i in range(2)] + \
              [wraw[0][:, 512 * i:512 * (i + 1)].bitcast(BF16).rearrange("p (m n) -> p m n", m=2) for i in range(2)]
        uT = scrA[:, 1024:3072].rearrange("p (g n) -> p g n", g=4)
        PT2 = [hT[:, 2 * i:2 * i + 2, :] for i in range(3)]
        atok = [scrA[:, 2560 + 256 * r:2560 + 256 * (r + 1)].bitcast(BF16).rearrange("p (t e) -> p t e", t=4)
                for r in range(2)]
        gv = [scrA[:, 3072:3584], scrA[:, 5632:6144], scrA[:, 6144:6656], scrA[:, 4608:5120]]
        vn = scrA[:, 3584:4608].bitcast(BF16).rearrange("p (t n) -> p t n", t=4)
        goT = scrA[:, 4608:5632].bitcast(BF16).rearrange("p (g n) -> p g n", g=4)
        Ocp = scrA[:, 3072:3072 + 1032]
        Ocp3 = Ocp.rearrange("p (a e) -> p a e", a=8)
        t2b = scrA[:, 6144:6656].rearrange("p (t e) -> p t e", t=4)
        ddall = scrA[:, 5632:6144].rearrange("p (t e) -> p t e", t=4)
        wb_raw = [wraw[0][:, :], wraw[1][:, :], scrA[:, 0:2048], scrA[:, 2048:4096], scrA[:, 4096:6144]]
        wbuf = [r.bitcast(BF16).rearrange("p (k n) -> p k n", k=8) for r in wb_raw]
        wbuf22 = [r.bitcast(BF16) for r in wb_raw]
        wf32 = [r.rearrange("p (k n) -> p k n", k=8) for r in wb_raw]
        x1 = arena[:, 0:8192].bitcast(F32).rearrange("p (t d) -> p t d", t=4)
        actT = arena[:, 8192:8192 + 22 * 512].rearrange("p (f n) -> p f n", f=22)
        cb = 8192 + 22 * 512
        L1G, L1B, L2G, L2B = [arena[:, cb + 2048 * i:cb + 2048 * (i + 1)].bitcast(F32) for i in range(4)]
        hT2 = arena[:, cb + 8192:cb + 12288].rearrange("p (k n) -> p k n", k=8)
        wbuf.append(hT2)
        sgbuf2 = arena[:, cb + 12288:cb + 14336].bitcast(F32)
        assert cb + 14336 <= 33792, cb
        wsin = [scrA[:, 0:1024]]
        wsout = [scrA[:, 1024:1536].bitcast(BF16)]
        Grow = [scrA[:, 1536:2560], scrA[:, 2560:3584]]
        wch = [scrA[:, 3584:4608], scrA[:, 4608:5632]]
        acc = scrA[:, 5632:6656]
        zcol = small[:, 0:1]
        lamc = small[:, 1:2]
        rr = small[:, 2:6]
        nhalf = small[:, 52:56]
        rr8 = small[:, 56:64]
        ss4 = small[:, 44:48]
        rs4 = small[:, 48:52]
        rstd4 = small[:, 8:12]
        nmr4 = small[:, 12:16]
        lnst = small[:, 16:28]
        lnmv = small[:, 28:30]
        lnr = small[:, 30:32]
        cact = small[:, 32:40]
        lsum = small[:, 40:44]

        bank_rot = {"i": 0}

        def KTkey(h, pos):
            return ("KT", h, pos)

        def cast_w(key, dst, src, nchunk):
            S.dma("pool", key,
                  lambda e: [e.dma_start(out=dst[k * 128:(k + 1) * 128, :], in_=src[k * 128:(k + 1) * 128, :])
                             for k in range(nchunk)], writes=[key], n=nchunk)

        def mk_a(e):
            e.memset(ident[:, :], 0.0)
            e.memset(small[:, 0:52], 0.0)
            e.memset(small[:, 52:56], -0.5)
            e.memset(small[:, 56:64], 0.0)
            e.memset(triB[:, :], 0.0)
            e.memset(ones_t[:, :], 1.0)
            return e.memset(arena[:, 0:33280].rearrange("p (kh e) -> p kh e", e=130)[:, :, 128:129], 1.0)
        S.op("pool", mk_a, writes=["ident", "small", "triB", "ones_t", "Vones", "cact0"])
        S.op("pool", lambda e: e.affine_select(out=ident[:, :], in_=ident[:, :], pattern=[[-1, 128]], compare_op=ALU.not_equal,
                                               fill=1.0, base=0, channel_multiplier=1), reads=["ident"], writes=["ident"])
        S.op("pool", lambda e: e.affine_select(out=triB[:, :], in_=triB[:, :], pattern=[[1, 128]], compare_op=ALU.is_ge,
                                               fill=NEG, base=0, channel_multiplier=-1), reads=["triB"], writes=["triB"])
        def cast_wi(g_):
            S.dma("pool", "wi_b%d" % g_,
                  lambda e, g_=g_: [e.dma_start(out=wi_b[k * 128:(k + 1) * 128, g_ * 512:(g_ + 1) * 512],
                                                in_=w_in[k * 128:(k + 1) * 128, g_ * 512:(g_ + 1) * 512]) for k in range(8)],
                  writes=[("wi_b", g_)], n=8)
        for g_ in (1, 2):
            cast_wi(g_)
        for g_ in ():
            S.dma("pool", "wi_b%d" % g_,
                  lambda e, g_=g_: [e.dma_start(out=wi_b[k * 128:(k + 1) * 128, g_ * 512:(g_ + 1) * 512],
                                                in_=w_in[k * 128:(k + 1) * 128, g_ * 512:(g_ + 1) * 512]) for k in range(8)],
                  writes=[("wi_b", g_)], n=8)
        S.dma("sp", "ccol", lambda e: e.dma_start(out=cact, in_=ccol), reads=["small"], writes=["cact"])
        S.dma("sp", "pbias", lambda e: e.dma_start(out=pbias[:, :], in_=pbias_d), writes=["pbias"])
        lamt = xt[3][:, 0:256]
        S.dma("sp", "lam4", lambda e: e.dma_start(out=lamt, in_=lam4.rearrange("a b -> (a b)").partition_broadcast(128)),
              writes=["lamt"])
        S.dma("sp", "subg", lambda e: e.dma_start(out=subg[:, :], in_=subln_g.rearrange("a b -> (a b)").partition_broadcast(128)),
              writes=["subg"])
        S.dma("sp", "sgg", lambda e: e.dma_start(out=sgg[:, :], in_=sgu_g.rearrange("a b -> (a b)").partition_broadcast(128)),
              writes=["sgg"])
        S.dma("sp", "sgb", lambda e: e.dma_start(out=sgb[:, :], in_=sgu_b.rearrange("a b -> (a b)").partition_broadcast(128)),
              writes=["sgb"])
        S.dma("sp", "bsb", lambda e: e.dma_start(out=bsb[:, :, :].rearrange("p g t -> p (g t)"),
                                                 in_=b_sp.rearrange("a b -> (a b)").partition_broadcast(128)),
              writes=["bsb"])
        wspt = xt[2][:, 0:512].rearrange("p (g s) -> p g s", g=4)
        S.dma("sp", "wsp", lambda e: e.dma_start(out=wspt, in_=w_sp.rearrange("g t s -> t g s")), writes=["wspt"])

        S.op("dve", lambda e: e.tensor_copy(out=identb[:, :], in_=ident[:, :]), reads=["ident"], writes=["identb"])
        def mask_ws(e):
            r = None
            for g in range(4):
                r = e.affine_select(out=wspt[:, g, :], in_=wspt[:, g, :], pattern=[[-1, 128]], compare_op=ALU.is_ge,
                                    fill=0.0, base=0, channel_multiplier=1)
            return r
        S.op("pool", mask_ws, reads=["wspt"], writes=["wspt"])
        def tr_ws(e):
            r = None
            for g in range(4):
                r = e.transpose(out=ps[0][:, g * 128:(g + 1) * 128], in_=wspt[:, g, :], identity=ident[:, :])
            return r
        S.op("pe", tr_ws, reads=["wspt", "ident"], writes=["B0"])
        S.op("dve", lambda e: e.tensor_copy(out=wsT[:, :, :].rearrange("p g t -> p (g t)"), in_=ps[0][:, :]),
             reads=["B0"], writes=["wsT"])
        S.op("dve", lambda e: e.tensor_scalar(out=subg[:, :], in0=subg[:, :], scalar1=float((1.0 - LAM_INIT) * np.sqrt(128.0)),
                                              scalar2=None, op0=ALU.mult), reads=["subg"], writes=["subg"])
        lam3 = lamt.rearrange("p (a b) -> p a b", a=4)
        def lam_a(e):
            e.tensor_tensor(out=lam3[:, 0, :], in0=lam3[:, 0, :], in1=lam3[:, 1, :], op=ALU.mult)
            return e.tensor_tensor(out=lam3[:, 2, :], in0=lam3[:, 2, :], in1=lam3[:, 3, :], op=ALU.mult)
        S.op("dve", lam_a, reads=["lamt"], writes=["lamt"])
        def lam_b(e):
            e.reduce_sum(out=lsum[:, 0:1], in_=lam3[:, 0, :], axis=AX.X)
            return e.reduce_sum(out=lsum[:, 1:2], in_=lam3[:, 2, :], axis=AX.X)
        S.op("dve", lam_b, reads=["lamt", "small"], writes=["lsum"])
        S.op("act", lambda e: e.activation(out=lsum[:, 2:4], in_=lsum[:, 0:2], func=AF.Exp), reads=["lsum"], writes=["lsum2"])
        S.op("dve", lambda e: e.scalar_tensor_tensor(out=lamc, in0=lsum[:, 2:3], scalar=LAM_INIT, in1=lsum[:, 3:4],
                                                     op0=ALU.add, op1=ALU.subtract), reads=["lsum2"], writes=["lamc"])
        S.op("act", lambda e: e.activation(out=cact, in_=cact, func=AF.Silu), reads=["cact", "cact0"], writes=["cact"])
        modrow = acc

        def mod_dma(part, kc, q="sp"):
            b = kc % 2
            S.dma(q, ("wch%d" if q == "sp" else "pwch%d") % b,
                  lambda e: e.dma_start(out=wch[b], in_=w_ada[kc * 128:(kc + 1) * 128, part * 1024:(part + 1) * 1024]),
                  writes=["wch%d" % b])

        def mod_step(part, kc, q="sp", dma=True):
            b = kc % 2
            if dma:
                mod_dma(part, kc, q)
            if kc == 0:
                S.op("dve", lambda e: e.tensor_scalar(out=acc, in0=wch[b], scalar1=cact[:, kc:kc + 1], scalar2=None, op0=ALU.mult),
                     reads=["wch%d" % b, "cact"], writes=["acc"])
            else:
                S.op("dve", lambda e: e.scalar_tensor_tensor(out=acc, in0=wch[b], scalar=cact[:, kc:kc + 1], in1=acc,
                                                             op0=ALU.mult, op1=ALU.add),
                     reads=["wch%d" % b, "cact", "acc"], writes=["acc"])

        def mod_bias_dma(part, q="sp"):
            S.dma(q, "wch0" if q == "sp" else "pwch0", lambda e: e.dma_start(out=wch[0][0:1, :], in_=b_ada[0:1, part * 1024:(part + 1) * 1024]),
                  writes=["wch0"])

        def mod_finish(part, banks, q="sp", dma=True):
            if dma:
                mod_bias_dma(part, q)
            S.op("dve", lambda e: e.tensor_tensor(out=acc[0:1, :], in0=acc[0:1, :], in1=wch[0][0:1, :], op=ALU.add),
                 reads=["acc", "wch0"], writes=["acc"])
            addc = 0.0 if part in (0, 3) else 1.0
            for half in range(2):
                bk = banks[half]
                S.op("pe", lambda e, half=half, bk=bk: e.matmul(out=ps[bk][:, :], lhsT=ones_t[:, :],
                                                               rhs=acc[:, half * 512:(half + 1) * 512], start=True, stop=True),
                     reads=["acc", "ones_t"], writes=["B%d" % bk])
            for half in range(2):
                bk = banks[half]
                S.op("dve", lambda e, half=half, bk=bk: e.tensor_scalar(out=modrow[:, half * 512:(half + 1) * 512], in0=ps[bk][:, :],
                                                                        scalar1=addc, scalar2=None, op0=ALU.add),
                     reads=["B%d" % bk], writes=["acc"])
            if part in (2, 5):
                S.op("dve", lambda e, gi=part // 3: e.tensor_copy(out=Grow[gi], in_=modrow), reads=["acc"], writes=["Grow%d" % (part // 3)])
            for half in range(2):
                bk = banks[half]
                def trm(e, half=half, bk=bk):
                    r = None
                    for k in range(4):
                        kc = half * 4 + k
                        r = e.transpose(out=ps[bk][:, k * 128:(k + 1) * 128], in_=modrow[:, kc * 128:(kc + 1) * 128],
                                        identity=ident[:, :])
                    return r
                S.op("pe", trm, reads=["acc", "ident"], writes=["B%d" % bk])
                S.op("dve", lambda e, half=half, bk=bk, part=part: e.tensor_copy(
                    out=cols[:, part, half * 4:half * 4 + 4],
                    in_=ps[bk][:, :].rearrange("p (k c) -> p k c", c=128)[:, :, 0]),
                    reads=["B%d" % bk], writes=["cols"])

        for part in (0, 1, 5):
            for kc in range(8):
                mod_step(part, kc)
            mod_finish(part, [0, 1])
        sh1c, sc1c, g1c, sh2c, sc2c, g2c = [cols[:, i, :] for i in range(6)]
        S.barrier()

        wkeys = {id(wi_b): "wi_b", id(wo_b): "wo_b", id(wg_b): "wg_b", id(wu_b): "wu_b", id(wd_b): "wd_b"}

        def load_w(i, src, k0, nk, n0, ncols, q="sp"):
            key = "wbuf%d" % i
            srcv = src.rearrange("(k p) n -> p k n", p=128)[:, k0:k0 + nk, n0:n0 + ncols]
            wk = wkeys[id(src)]
            if wk in ("wo_b", "wd_b"):
                rd = [(wk, kc) for kc in range(k0, k0 + nk)]
            elif wk == "wi_b":
                rd = [(wk, n0 // 512)]
            else:
                rd = [wk]
            S.dma(q, key, lambda e: e.dma_start(out=wbuf[i][:, 0:nk, 0:ncols], in_=srcv), reads=rd, writes=[key])

        def load_x_tile(pos, q="sp"):
            for tb in range(4):
                r0 = pos * 512 + tb * 128
                S.dma(q, "xt%d" % tb, lambda e, tb=tb, r0=r0: e.dma_start(out=xt[tb][:, :], in_=xs[r0:r0 + 128, :]),
                      writes=["xt%d" % tb])

        def next_bank(lst):
            k = tuple(lst)
            i = bank_rot.get(k, 0)
            bank_rot[k] = i + 1
            return lst[i % len(lst)]

        evac_rot = {"i": 0}

        def transpose_mod(srcs, srckeys, scc, shc, banks, engines, dst=None, dk="hT", kcs=range(8)):
            if dst is None:
                dst = hT
            for kc in kcs:
                bk = next_bank(banks)
                def tr(e, kc=kc, bk=bk):
                    r = None
                    for tb in range(4):
                        r = e.transpose(out=ps[bk][:, tb * 128:(tb + 1) * 128], in_=srcs[tb][:, kc * 128:(kc + 1) * 128],
                                        identity=ident[:, :])
                    return r
                S.op("pe", tr, reads=list(srckeys) + ["ident"], writes=["B%d" % bk])
                eng = engines[evac_rot["i"] % len(engines)]
                evac_rot["i"] += 1
                if eng == "act":
                    S.op("act", lambda e, kc=kc, bk=bk: e.activation(out=dst[:, kc, :], in_=ps[bk][:, :], func=AF.Identity,
                                                                     bias=shc[:, kc:kc + 1], scale=scc[:, kc:kc + 1]),
                         reads=["B%d" % bk, "cols"], writes=[(dk, kc)])
                else:
                    S.op("dve", lambda e, kc=kc, bk=bk: e.tensor_scalar(out=dst[:, kc, :], in0=ps[bk][:, :],
                                                                        scalar1=scc[:, kc:kc + 1], scalar2=shc[:, kc:kc + 1],
                                                                        op0=ALU.mult, op1=ALU.add),
                         reads=["B%d" % bk, "cols"], writes=[(dk, kc)])

        hTkeys = [("hT", kc) for kc in range(8)]

        def proj_fm(wi, c, bk, src=None, sk="hT"):
            if src is None:
                src = hT
            def fn(e):
                r = None
                for kc in range(8):
                    r = e.matmul(out=ps[bk][:, :], lhsT=wbuf[wi][:, kc, c * 128:(c + 1) * 128], rhs=src[:, kc, :],
                                 start=(kc == 0), stop=(kc == 7))
                return r
            S.op("pe", fn, reads=[(sk, kc) for kc in range(8)] + ["wbuf%d" % wi], writes=["B%d" % bk])

        def proj_tm(wi, tb, bk):
            def fn(e):
                r = None
                for kc in range(8):
                    r = e.matmul(out=ps[bk][:, :], lhsT=hT[:, kc, tb * 128:(tb + 1) * 128], rhs=wbuf[wi][:, kc, :],
                                 start=(kc == 0), stop=(kc == 7))
                return r
            S.op("pe", fn, reads=hTkeys + ["wbuf%d" % wi], writes=["B%d" % bk])

        xkeys = ["xt%d" % i for i in range(4)]

        load_w(0, wi_b, 0, 8, 512, 512)
        load_w(1, wi_b, 0, 8, 1024, 512)
        P0_TP = [0, 1, 2]
        P0_PJ = [7, 3, 4, 5, 6]
        npos0 = 2 * nslot_a
        def scale_chunk(idx):
            gi, kc = (1, idx) if idx < 22 else (0, idx - 22)
            srcw, dstw, name = ((w_out, wo_b, "wo_b"), (w_down, wd_b, "wd_b"))[gi]
            b = 0
            S.dma("pool", "wsin%d" % b, lambda e: e.dma_start(out=wsin[b], in_=srcw[kc * 128:(kc + 1) * 128, :]), writes=["wsin%d" % b])
            S.op("pool", lambda e: e.tensor_tensor(out=wsout[b], in0=wsin[b], in1=Grow[gi], op=ALU.mult),
                 reads=["wsin%d" % b, "Grow%d" % gi], writes=["wsout%d" % b])
            S.dma("pool", "wsout%d" % b, lambda e: e.dma_start(out=dstw[kc * 128:(kc + 1) * 128, :], in_=wsout[b]),
                  reads=["wsout%d" % b], writes=[(name, kc)])
        sc_i = 0
        md_i = 0
        for pos in range(npos0):
            load_x_tile(pos)
            if pos in (1, 3, 5):
                cast_wi({1: 0, 3: 4, 5: 3}[pos])
            for _ in range(2):
                if sc_i < 30:
                    scale_chunk(sc_i)
                    sc_i += 1
            k_, r_ = pos // 5, pos % 5
            fin_part = (1 + k_) if (r_ == 0 and 1 <= k_ <= 3) else None
            if k_ < 3:
                part_ = 2 + k_
                if r_ >= 1:
                    for kc_ in (2 * r_ - 2, 2 * r_ - 1):
                        mod_step(part_, kc_, q="pool", dma=False)
                if r_ == 0 and fin_part is not None:
                    pass
                elif r_ <= 3:
                    for kc_ in (2 * r_, 2 * r_ + 1):
                        mod_dma(part_, kc_, q="pool")
                else:
                    mod_bias_dma(part_, q="pool")
            transpose_mod([x[:, :] for x in xt], xkeys, sc1c, sh1c, P0_TP, ["act"], kcs=range(0, 4))
            for h in range(4):
                def k1(e, h=h):
                    r = None
                    for kc in range(4):
                        r = e.matmul(out=ps[3 + h][:, :], lhsT=wbuf[0][:, kc, h * 128:(h + 1) * 128], rhs=hT[:, kc, :],
                                     start=(kc == 0), stop=False)
                    return r
                S.op("pe", k1, reads=[("hT", kc) for kc in range(4)] + ["wbuf0"], writes=["B%d" % (3 + h)])
            if fin_part is not None:
                mod_finish(fin_part, [7, 2], q="pool", dma=False)
                if k_ < 3:
                    for kc_ in (0, 1):
                        mod_dma(2 + k_, kc_, q="pool")
            transpose_mod([x[:, :] for x in xt], xkeys, sc1c, sh1c, P0_TP, ["act"], kcs=range(4, 8))
            for h in range(4):
                bk = 3 + h
                def k2(e, h=h, bk=bk):
                    r = None
                    for kc in range(4, 8):
                        r = e.matmul(out=ps[bk][:, :], lhsT=wbuf[0][:, kc, h * 128:(h + 1) * 128], rhs=hT[:, kc, :],
                                     start=False, stop=(kc == 7))
                    return r
                S.op("pe", k2, reads=[("hT", kc) for kc in range(4, 8)] + ["wbuf0"], writes=["B%d" % bk])
                dst = KT[:, h, pos * 512:(pos + 1) * 512]
                if False:
                    S.op("act", lambda e, dst=dst, bk=bk: e.activation(out=dst, in_=ps[bk][:, :], func=AF.Identity),
                         reads=["B%d" % bk], writes=[KTkey(h, pos)])
                else:
                    S.op("dve", lambda e, dst=dst, bk=bk: e.tensor_copy(out=dst, in_=ps[bk][:, :]),
                         reads=["B%d" % bk], writes=[KTkey(h, pos)])
            for tb in range(4):
                bk = next_bank(P0_PJ)
                proj_tm(1, tb, bk)
                kb = pos * 4 + tb
                dst = Vt[:, kb, :, 0:128]
                src = ps[bk][:, :].rearrange("p (h e) -> p h e", h=4)
                if tb % 2 == 0:
                    S.op("act", lambda e, dst=dst, src=src: e.activation(out=dst, in_=src, func=AF.Identity),
                         reads=["B%d" % bk], writes=[("V", kb)])
                else:
                    S.op("dve", lambda e, dst=dst, src=src: e.tensor_copy(out=dst, in_=src),
                         reads=["B%d" % bk], writes=[("V", kb)])
        while sc_i < 30:
            scale_chunk(sc_i)
            sc_i += 1
        S.barrier()

        GEN = [1, 2, 3, 4]
        def Oacc(m, tb):
            a = m * 4 + tb
            return ps[5 + a // 3][:, (a % 3) * 129:(a % 3) * 129 + 129], ("O", a)
        Ukey = lambda i: ("U", i)
        GVK = [[("gv", 0)], [("gv", 1)] + [("dd", t) for t in range(4)], ["t2b"], [("goT", 0), ("goT", 1)]]

        def mixT_view(j, c):
            return KT[:, c // 2, 1024 * j + (c % 2) * 512:1024 * j + (c % 2) * 512 + 512], KTkey(c // 2, 2 * j + (c % 2))

        deferred_casts = [(key, dst, src, k) for k in range(8) for (key, dst, src) in (("wg_b", wg_b, w_gate), ("wu_b", wu_b, w_up))]

        def issue_cast():
            if deferred_casts:
                key, dst, src, k = deferred_casts.pop(0)
                S.dma("pool", key, lambda e: e.dma_start(out=dst[k * 128:(k + 1) * 128, :], in_=src[k * 128:(k + 1) * 128, :]),
                      writes=[key])
        load_w(1, wi_b, 0, 8, 2048, 512)
        def zq(e):
            e.memset(QTall, 0.0)
            return e.memset(QTall2, 0.0)
        S.op("pool", zq, writes=[("QT", h) for h in range(4)])

        def qproj(h, banks, eng="dve"):
            bk = next_bank(banks)
            def fnq(e, h=h, bk=bk):
                r = None
                for kc in range(8):
                    r = e.matmul(out=ps[bk][:, :], lhsT=wqx[:, kc, h * 128:(h + 1) * 128], rhs=hT[:, kc, :],
                                 start=(kc == 0), stop=(kc == 7))
                return r
            S.op("pe", fnq, reads=hTkeys + ["xt2", "xt3"], writes=["B%d" % bk])
            hb = h
            if eng == "act":
                def cp(e, bk=bk, hb=hb):
                    e.activation(out=QTp[hb][0:64, 0, :], in_=ps[bk][0:64, :], func=AF.Identity)
                    return e.activation(out=QTp[hb][64:128, 1, :], in_=ps[bk][64:128, :], func=AF.Identity)
            else:
                def cp(e, bk=bk, hb=hb):
                    e.tensor_copy(out=QTp[hb][0:64, 0, :], in_=ps[bk][0:64, :])
                    return e.tensor_copy(out=QTp[hb][64:128, 1, :], in_=ps[bk][64:128, :])
            S.op(eng, cp, reads=["B%d" % bk], writes=[("QT", hb)])

        def slot_pre(j, evac):
            transpose_mod([x[:, :] for x in xt], xkeys, sc1c, sh1c, GEN, evac)
            for tb in range(4):
                bk = next_bank(GEN)
                proj_tm(1, tb, bk)
                gvb = gv[tb]
                gk = GVK[tb]
                stats = stats_all[:, tb, :].rearrange("p (g s) -> p g s", g=4)
                mv = mv_all[:, tb, :].rearrange("p (g s) -> p g s", g=4)
                rstd4 = lnsm[:, tb, 0:4]
                nmr4 = lnsm[:, tb, 4:8]
                sk_, mk2_, rk_, nk_ = ("stats", tb), ("mv", tb), ("rstd4", tb), ("nmr4", tb)
                S.op("act", lambda e, gvb=gvb, bk=bk: e.activation(out=gvb, in_=ps[bk][:, :], func=AF.Gelu_apprx_tanh),
                     reads=["B%d" % bk], writes=gk)
                def st_(e, gvb=gvb, stats=stats):
                    r = None
                    for g in range(4):
                        r = e.bn_stats(out=stats[:, g, :], in_=gvb[:, g * 128:(g + 1) * 128])
                    return r
                S.op("dve", st_, reads=gk, writes=[sk_])
                def ag_(e, stats=stats, mv=mv):
                    r = None
                    for g in range(4):
                        r = e.bn_aggr(out=mv[:, g, :], in_=stats[:, g, :])
                    return r
                S.op("dve", ag_, reads=[sk_], writes=[mk2_])
                S.op("dve", lambda e, mv=mv, rstd4=rstd4: e.tensor_scalar(out=rstd4, in0=mv[:, :, 1], scalar1=LN_EPS, scalar2=None,
                                                                         op0=ALU.add), reads=[mk2_], writes=[rk_])
                S.op("pool", lambda e, rstd4=rstd4: e.tensor_tensor(out=rstd4, in0=rstd4, in1=nhalf, op=ALU.pow),
                     reads=[rk_], writes=[rk_])
                S.op("dve", lambda e, mv=mv, rstd4=rstd4, nmr4=nmr4: e.scalar_tensor_tensor(out=nmr4, in0=mv[:, :, 0], scalar=-1.0, in1=rstd4,
                                                                                        op0=ALU.mult, op1=ALU.mult),
                     reads=[mk2_, rk_], writes=[nk_])
                def nrm_(e, gvb=gvb, rstd4=rstd4, nmr4=nmr4):
                    r = None
                    for g in range(4):
                        r = e.tensor_scalar(out=gvb[:, g * 128:(g + 1) * 128], in0=gvb[:, g * 128:(g + 1) * 128],
                                            scalar1=rstd4[:, g:g + 1], scalar2=nmr4[:, g:g + 1], op0=ALU.mult, op1=ALU.add)
                    return r
                S.op("dve", nrm_, reads=gk + [rk_, nk_], writes=gk)
                S.op("pool", lambda e, gvb=gvb: e.tensor_tensor(out=gvb, in0=gvb, in1=sgg[:, :], op=ALU.mult),
                     reads=gk + ["sgg"], writes=gk)
                S.op("pool", lambda e, gvb=gvb, tb=tb: e.tensor_tensor(out=vn[:, tb, :], in0=gvb, in1=sgb[:, :], op=ALU.add),
                     reads=gk + ["sgb"], writes=[("vn", tb)])

            S.dma("sp", "wux", lambda e: e.dma_start(out=wux[:, :, :], in_=wi_b.rearrange("(k p) n -> p k n", p=128)[:, :, 1536:2048]),
                  reads=[("wi_b", 3)], writes=["xt0", "xt1"])
            S.dma("sp", "wqx", lambda e: e.dma_start(out=wqx[:, :, :], in_=wi_b.rearrange("(k p) n -> p k n", p=128)[:, :, 0:512]),
                  reads=[("wi_b", 0)], writes=["xt2", "xt3"])

        def slot_mid(j):
            for g in range(4):
                bk = next_bank(GEN)
                def fn(e, g=g, bk=bk):
                    r = None
                    for kc in range(8):
                        r = e.matmul(out=ps[bk][:, :], lhsT=wux[:, kc, g * 128:(g + 1) * 128], rhs=hT[:, kc, :],
                                     start=(kc == 0), stop=(kc == 7))
                    return r
                S.op("pe", fn, reads=hTkeys + ["xt0", "xt1"], writes=["B%d" % bk])
                S.op("act", lambda e, g=g, bk=bk: e.activation(out=uT[:, g, :], in_=ps[bk][:, :], func=AF.Gelu_apprx_tanh),
                     reads=["B%d" % bk], writes=[Ukey(2 * g), Ukey(2 * g + 1)])
            for h in range(4):
                qproj(h, GEN, eng="act")
            if j > 0:
                load_x_tile(2 * (j - 1))
            def sgu_group(g):
                bk = 0
                def sg_(e, g=g, bk=bk):
                    r = None
                    for tb in range(4):
                        r = e.matmul(out=ps[bk][:, tb * 128:(tb + 1) * 128], lhsT=vn[:, tb, g * 128:(g + 1) * 128],
                                     rhs=wsT[:, g, :], start=True, stop=True)
                    return r
                S.op("pe", sg_, reads=[("vn", t) for t in range(4)] + ["wsT"], writes=["B%d" % bk])
                gtmp = gv[g % 2]
                gk = GVK[g % 2]
                S.op("dve", lambda e, g=g, bk=bk, gtmp=gtmp: e.tensor_tensor(
                    out=gtmp.rearrange("p (t n) -> p t n", t=4), in0=ps[bk][:, :].rearrange("p (t n) -> p t n", t=4),
                    in1=bsb[:, g:g + 1, :].to_broadcast([128, 4, 128]), op=ALU.add),
                    reads=["B%d" % bk, "bsb"], writes=gk)
                S.op("dve", lambda e, g=g, gtmp=gtmp: e.tensor_tensor(out=goT[:, g, :], in0=gtmp, in1=uT[:, g, :], op=ALU.mult),
                     reads=gk + [Ukey(2 * g), Ukey(2 * g + 1)], writes=[("goT", g)])
            return [(lambda g=g: sgu_group(g)) for g in range(4)]

        def attention_head(j, h, hooks):
            nkb = 8 * (j + 1)
            hb = h
            def qk(i, h=h, j=j, hb=hb):
                kb = i
                d = kb - 8 * j
                diag = 0 <= d < 4
                c0 = 128 * d if diag else 0
                sb = i % 2
                kw = KT[:, h, kb * 128:(kb + 1) * 128]
                def fn(e):
                    r = None
                    for m in range(2):
                        if diag:
                            e.matmul(out=S2[sb][:, m, c0:c0 + 128], lhsT=kw, rhs=QTp[hb][:, m, c0:c0 + 128], start=True, stop=False)
                            r = e.matmul(out=S2[sb][:, m, c0:c0 + 128], lhsT=identb[:, :], rhs=triB[:, :], start=False, stop=True)
                            if c0 + 128 < 512:
                                r = e.matmul(out=S2[sb][:, m, c0 + 128:512], lhsT=kw, rhs=QTp[hb][:, m, c0 + 128:512],
                                             start=True, stop=True)
                        else:
                            r = e.matmul(out=S2[sb][:, m, :], lhsT=kw, rhs=QTp[hb][:, m, :], start=True, stop=True)
                    return r
                S.op("pe", fn, reads=[KTkey(h, kb // 4), ("QT", hb), "identb", "triB"], writes=S2K[sb])
                pti = i % 3
                bias = pbias[:, j:j + 1] if d >= 4 else zcol
                S.op("act", lambda e: e.activation(out=PT2[pti][:, :, c0:512], in_=S2[sb][:, :, c0:512], func=AF.Exp,
                                                   bias=bias, scale=0.125),
                     reads=S2K[sb] + ["pbias", "small"], writes=[("hT", 2 * pti), ("hT", 2 * pti + 1)])
            def pv(i, h=h, j=j, nkb=nkb):
                kb = i
                d = kb - 8 * j
                tb0 = d if 0 <= d < 4 else 0
                pti = i % 3
                def fn(e):
                    r = None
                    for m in range(2):
                        for tb in range(tb0, 4):
                            o, _ = Oacc(m, tb)
                            r = e.matmul(out=o, lhsT=PT2[pti][:, m, tb * 128:(tb + 1) * 128], rhs=Vt[:, kb, h, 0:129],
                                         start=(kb == 0 and (m * 4 + tb) % 3 == 0), stop=(kb == nkb - 1), skip_group_check=True)
                    return r
                S.op("pe", fn, reads=[("hT", 2 * pti), ("hT", 2 * pti + 1), ("V", kb)],
                     writes=[Oacc(m, tb)[1] for m in range(2) for tb in range(tb0, 4)])

            n = nkb
            qk(0)
            for i in range(n):
                if i + 1 < n:
                    qk(i + 1)
                pv(i)
                for hk_ in hooks.get(i, ()):
                    hk_()
            ar = h % 2
            OK_ = [("O", a_) for a_ in range(8)]
            OCK = [("gv", 0)] + [("vn", t) for t in range(4)]
            DDK = GVK[1]
            def ocp(e):
                e.tensor_copy(out=Ocp[:, 0:387], in_=ps[5][:, 0:387])
                e.tensor_copy(out=Ocp[:, 387:774], in_=ps[6][:, 0:387])
                return e.tensor_copy(out=Ocp[:, 774:1032], in_=ps[7][:, 0:258])
            S.op("dve", ocp, reads=OK_, writes=OCK)
            S.op("dve", lambda e: e.reciprocal(out=rr8, in_=Ocp3[:, :, 128]), reads=OCK, writes=["rr8"])
            S.op("dve", lambda e: e.tensor_scalar(out=rr, in0=rr8[:, 4:8], scalar1=lamc, scalar2=None, op0=ALU.mult),
                 reads=["rr8", "lamc"], writes=["rr"])
            S.op("dve", lambda e: e.tensor_tensor(out=t2b, in0=Ocp3[:, 4:8, 0:128],
                                                  in1=rr.unsqueeze(2).to_broadcast([128, 4, 128]), op=ALU.mult),
                 reads=OCK + ["rr"], writes=["t2b"])
            S.op("dve", lambda e: e.tensor_tensor(out=ddall, in0=Ocp3[:, 0:4, 0:128],
                                                  in1=rr8[:, 0:4].unsqueeze(2).to_broadcast([128, 4, 128]), op=ALU.mult),
                 reads=OCK + ["rr8"], writes=DDK)
            S.op("dve", lambda e: e.tensor_tensor(out=ddall, in0=ddall, in1=t2b, op=ALU.subtract),
                 reads=DDK + ["t2b"], writes=DDK)
            S.op("dve", lambda e: e.tensor_tensor(out=t2b, in0=ddall, in1=ddall, op=ALU.mult), reads=DDK, writes=["t2b"])
            S.op("dve", lambda e: e.reduce_sum(out=ss4, in_=t2b, axis=AX.X), reads=["t2b"], writes=["ss4"])
            S.op("dve", lambda e: e.tensor_scalar(out=rs4, in0=ss4, scalar1=float(128.0 * RMS_EPS), scalar2=None,
                                                  op0=ALU.add), reads=["ss4"], writes=["rs4"])
            S.op("pool", lambda e: e.tensor_tensor(out=rs4, in0=rs4, in1=nhalf, op=ALU.pow), reads=["rs4"], writes=["rs4"])
            S.op("dve", lambda e: e.tensor_tensor(out=ddall, in0=ddall, in1=rs4.unsqueeze(2).to_broadcast([128, 4, 128]),
                                                  op=ALU.mult), reads=DDK + ["rs4"], writes=DDK)
            S.op("dve", lambda e, ar=ar: e.tensor_tensor(out=atok[ar], in0=ddall,
                                                         in1=subg[:, :].unsqueeze(1).to_broadcast([128, 4, 128]), op=ALU.mult),
                 reads=DDK + ["subg"], writes=[Ukey(6 + ar)])

            def fin(h=h, ar=ar, j=j):
                bk = 0
                def trn(e):
                    r = None
                    for tb in range(4):
                        r = e.transpose(out=psb[bk][:, tb * 128:(tb + 1) * 128], in_=atok[ar][:, tb, :], identity=identb[:, :])
                    return r
                S.op("pe", trn, reads=[Ukey(6 + ar), "identb"], writes=["B%d" % bk])
                mv_, mk_ = mixT_view(j, h)
                S.op("dve", lambda e: e.tensor_copy(out=mv_, in_=psb[bk][:, 0:512]), reads=["B%d" % bk], writes=[mk_])

            return fin

        load_x_tile(2 * (nslot_a - 1))
        slot_pre(nslot_a - 1, ["act", "dve"])
        sgu_hooks = slot_mid(nslot_a - 1)
        for j in range(nslot_a - 1, -1, -1):
            pend = None
            for h in range(4):
                hooks = {}
                if h == 0:
                    for g in range(4):
                        hooks.setdefault(3 + g, []).append(sgu_hooks[g])
                hooks.setdefault(5, []).append(issue_cast)
                if pend is not None:
                    hooks.setdefault(6, []).append(pend)
                pend = attention_head(j, h, hooks)
            for g in range(4):
                mv_, mk_ = mixT_view(j, 4 + g)
                S.op("pool", lambda e, g=g, mv_=mv_: e.tensor_copy(out=mv_, in_=goT[:, g, :]), reads=[("goT", g)], writes=[mk_])
            if j > 0:
                slot_pre(j - 1, ["act"])
            pend()
            if j > 0:
                sgu_hooks = slot_mid(j - 1)
        while deferred_casts:
            issue_cast()
        S.barrier()

        if debug == "mixT":
            for c in range(8):
                mv_, mk_ = mixT_view(0, c)
                S.op("dve", lambda e, mv_=mv_: e.tensor_copy(out=xt[0][:, 0:512], in_=mv_), reads=[mk_, "xt0"], writes=["xt0"])
                S.dma("sp", "dbg", lambda e, c=c: e.dma_start(out=dbg[:, c * 512:(c + 1) * 512], in_=xt[0][:, 0:512]), reads=["xt0"], writes=["dbgd"])

        wrot = {"i": 0}

        def next_w():
            i = wrot["i"] % 6
            wrot["i"] += 1
            return i

        PO = [0, 1]
        PD = [2, 3]
        PG = [4, 5]
        PU = [6, 7]
        bufX = [[xt[tb][:, :] for tb in range(4)], [x1[:, tb, :] for tb in range(4)]]
        bufK = [["xt%d" % tb for tb in range(4)], ["x1_%d" % tb for tb in range(4)]]
        hTs = [(hT, "hT"), (hT, "hT")]
        sgs = [sgbuf2[:, 0:512], sgbuf2[:, 512:1024]]

        def layer_norm_inplace(buf, bkey, Gt, Bt, gkey, bkey2):
            def st_(e):
                e.bn_stats(out=lnst[:, 0:6], in_=buf[:, 0:512])
                return e.bn_stats(out=lnst[:, 6:12], in_=buf[:, 512:1024])
            S.op("dve", st_, reads=[bkey], writes=["lnst"])
            S.op("dve", lambda e: e.bn_aggr(out=lnmv, in_=lnst), reads=["lnst"], writes=["lnmv"])
            S.op("dve", lambda e: e.tensor_scalar(out=lnr[:, 0:1], in0=lnmv[:, 1:2], scalar1=LN_EPS, scalar2=None,
                                                  op0=ALU.add), reads=["lnmv"], writes=["lnr0"])
            S.op("pool", lambda e: e.tensor_tensor(out=lnr[:, 0:1], in0=lnr[:, 0:1], in1=nhalf[:, 0:1], op=ALU.pow),
                 reads=["lnr0"], writes=["lnr0"])
            S.op("dve", lambda e: e.scalar_tensor_tensor(out=lnr[:, 1:2], in0=lnmv[:, 0:1], scalar=-1.0, in1=lnr[:, 0:1],
                                                         op0=ALU.mult, op1=ALU.mult), reads=["lnmv", "lnr0"], writes=["lnr1"])
            S.op("act", lambda e: e.activation(out=buf, in_=buf, func=AF.Identity, bias=lnr[:, 1:2], scale=lnr[:, 0:1]),
                 reads=[bkey, "lnr0", "lnr1"], writes=[bkey])
            S.op("dve", lambda e: e.tensor_tensor(out=buf, in0=buf, in1=Gt, op=ALU.mult), reads=[bkey, gkey], writes=[bkey])
            S.op("dve", lambda e: e.tensor_tensor(out=buf, in0=buf, in1=Bt, op=ALU.add), reads=[bkey, bkey2], writes=[bkey])

        def stage_X(j):
            s_ = j % 2
            for tb in range(4):
                r0 = 2 * j * 512 + tb * 128
                buf, bkey = bufX[s_][tb], bufK[s_][tb]
                S.dma("pool", "p_" + bkey, lambda e, buf=buf, r0=r0: e.dma_start(out=buf, in_=xs[r0:r0 + 128, :]), writes=[bkey])

        def stage_A(j):
            s_ = j % 2
            wo_i = []
            for dh in range(2):
                i = next_w()
                load_w(i, wo_b, 0, 8, dh * 512, 512)
                wo_i.append(i)
            for tb in range(4):
                buf, bkey = bufX[s_][tb], bufK[s_][tb]
                for dh in range(2):
                    bk = next_bank(PO)
                    def mo(e, dh=dh, bk=bk, tb=tb, j=j, wo_i=tuple(wo_i)):
                        r = None
                        for c in range(8):
                            mvw, _ = mixT_view(j, c)
                            r = e.matmul(out=ps[bk][:, :], lhsT=mvw[:, tb * 128:(tb + 1) * 128], rhs=wbuf[wo_i[dh]][:, c, :],
                                         start=(c == 0), stop=(c == 7))
                        return r
                    S.op("pe", mo, reads=[mixT_view(j, c)[1] for c in range(8)] + ["wbuf%d" % wo_i[dh]], writes=["B%d" % bk])
                    S.op("dve", lambda e, dh=dh, bk=bk, buf=buf: e.scalar_tensor_tensor(
                        out=buf[:, dh * 512:(dh + 1) * 512], in0=buf[:, dh * 512:(dh + 1) * 512], scalar=ALPHA, in1=ps[bk][:, :],
                        op0=ALU.mult, op1=ALU.add), reads=[bkey, "B%d" % bk], writes=[bkey])
                layer_norm_inplace(buf, bkey, L1G, L1B, "lnb0", "lnb1")

        def stage_T(j):
            s_ = j % 2
            transpose_mod(bufX[s_], bufK[s_], sc2c, sh2c, PO, ["act", "dve"], dst=hTs[s_][0], dk=hTs[s_][1])

        def stage_GU(j, hooks={}):
            s_ = j % 2
            hsrc, hk = hTs[s_]
            for fg in range(6):
                nf = 4 if fg < 5 else 2
                ig = next_w()
                load_w(ig, wg_b, 0, 8, fg * 512, nf * 128)
                iu = next_w()
                load_w(iu, wu_b, 0, 8, fg * 512, nf * 128)
                for fl in range(nf):
                    fc = fg * 4 + fl
                    bg = next_bank(PG)
                    bu = next_bank(PU)
                    proj_fm(ig, fl, bg, src=hsrc, sk=hk)
                    proj_fm(iu, fl, bu, src=hsrc, sk=hk)
                    sk = ("sg", fc % 2)
                    sgb_ = sgs[fc % 2]
                    S.op("act", lambda e, bg=bg, sgb_=sgb_: e.activation(out=sgb_, in_=ps[bg][:, :], func=AF.Silu),
                         reads=["B%d" % bg], writes=[sk])
                    S.op("dve", lambda e, fc=fc, bu=bu, sgb_=sgb_: e.tensor_tensor(out=actT[:, fc, :], in0=ps[bu][:, :], in1=sgb_,
                                                                                  op=ALU.mult),
                         reads=["B%d" % bu, sk], writes=[("actT", fc)])
                    if fc in hooks:
                        hooks[fc]()

        def stage_DN(j):
            s_ = j % 2
            for dh in range(2):
                for (k0, nk) in ((0, 8), (8, 8), (16, 6)):
                    i = next_w()
                    load_w(i, wd_b, k0, nk, dh * 512, 512)
                    for tb in range(4):
                        bk = dh * 4 + tb
                        def md(e, tb=tb, bk=bk, i=i, k0=k0, nk=nk):
                            r = None
                            for kk in range(nk):
                                fc = k0 + kk
                                r = e.matmul(out=ps[bk][:, :], lhsT=actT[:, fc, tb * 128:(tb + 1) * 128], rhs=wbuf[i][:, kk, :],
                                             start=(fc == 0), stop=(fc == 21))
                            return r
                        S.op("pe", md, reads=[("actT", fc) for fc in range(k0, k0 + nk)] + ["wbuf%d" % i], writes=["B%d" % bk])
                for tb in range(4):
                    buf, bkey = bufX[s_][tb], bufK[s_][tb]
                    bk = dh * 4 + tb
                    S.op("dve", lambda e, dh=dh, bk=bk, buf=buf: e.scalar_tensor_tensor(
                        out=buf[:, dh * 512:(dh + 1) * 512], in0=buf[:, dh * 512:(dh + 1) * 512], scalar=ALPHA, in1=ps[bk][:, :],
                        op0=ALU.mult, op1=ALU.add), reads=[bkey, "B%d" % bk], writes=[bkey])

        def stage_OUT(j, tb):
            s_ = j % 2
            buf, bkey = bufX[s_][tb], bufK[s_][tb]
            layer_norm_inplace(buf, bkey, L2G, L2B, "lnb2", "lnb3")
            r0 = j * 512 + tb * 128
            S.dma("pool", "st%d_%d" % (s_, tb), lambda e: e.dma_start(out=y[r0:r0 + 128, :], in_=buf),
                  reads=[bkey], writes=[("y", j, tb)])
            if j + 2 < nslot_b:
                r1 = 2 * (j + 2) * 512 + tb * 128
                S.dma("pool", "p_" + bkey, lambda e: e.dma_start(out=buf, in_=xs[r1:r1 + 128, :]), writes=[bkey])

        if nslot_b > 0:
            stage_X(0)
            for i, (tile_, src) in enumerate(((L1G, ln1_g), (L1B, ln1_b), (L2G, ln2_g), (L2B, ln2_b))):
                S.dma("pool", "lnb%d" % i, lambda e, tile_=tile_, src=src: e.dma_start(
                    out=tile_, in_=src.rearrange("a b -> (a b)").partition_broadcast(128)), writes=["lnb%d" % i])
            stage_A(0)
            stage_T(0)
        if nslot_b > 1:
            stage_X(1)
        for j in range(nslot_b):
            hooks = {}
            if j > 0:
                for tb in range(4):
                    hooks[2 + 5 * tb] = (lambda j=j, tb=tb: stage_OUT(j - 1, tb))
            stage_GU(j, hooks)
            if j + 1 < nslot_b:
                stage_A(j + 1)
            stage_DN(j)
            if j + 1 < nslot_b:
                stage_T(j + 1)
        for tb in range(4):
            stage_OUT(nslot_b - 1, tb)
        keys = [k for k in S.dma_sems if isinstance(k, str) and k.startswith("st")]
        if debug:
            keys.append("dbg")
        S.final_wait("sp", keys)
        S.final_wait("pool", keys)
        with nc.Block() as block:
            S.emit(block)
    return nc


_NC_CACHE = {}


def _core_layout(c):
    b, p = c // 2, c % 2
    own, partner, vis = [], [], []
    for j in range(NSLOT):
        o = 2 * j + ((j & 1) ^ p)
        pr = 4 * j + 1 - o
        own.append(o)
        partner.append(pr)
        vis.append(pr < o)
    return b, own, partner, vis


def make_in_maps(inp):
    x = np.asarray(inp["x"], dtype=np.float32)
    c = np.asarray(inp["c"], dtype=np.float32)
    sq = lambda k: np.ascontiguousarray(np.asarray(inp[k], dtype=np.float32)[0])
    shared = {
        "w_ada": sq("w_ada"), "b_ada": sq("b_ada").reshape(1, -1), "w_in": sq("w_in"),
        "lam4": np.ascontiguousarray(np.stack([sq("lambda_q1"), sq("lambda_k1"), sq("lambda_q2"), sq("lambda_k2")], 0)),
        "subln_g": sq("subln_g").reshape(1, -1), "sgu_ln_g": sq("sgu_ln_g").reshape(1, -1),
        "sgu_ln_b": sq("sgu_ln_b").reshape(1, -1), "w_spatial": sq("w_spatial"), "b_spatial": sq("b_spatial"),
        "w_out": sq("w_out"), "ln1_g": sq("ln1_g").reshape(1, -1), "ln1_b": sq("ln1_b").reshape(1, -1),
        "w_gate": sq("w_gate"), "w_up": sq("w_up"), "w_down": sq("w_down"),
        "ln2_g": sq("ln2_g").reshape(1, -1), "ln2_b": sq("ln2_b").reshape(1, -1),
    }
    maps = []
    for core in range(8):
        b, own, partner, vis = _core_layout(core)
        xb = x[b].reshape(16, 512, D)
        order = []
        for j in range(NSLOT):
            order += [own[j], partner[j]]
        xs = np.ascontiguousarray(xb[order].reshape(SEQ, D))
        pb = np.zeros((128, 8), np.float32)
        for j in range(NSLOT):
            if not vis[j]:
                pb[:, j] = NEG
        m = dict(shared)
        m["xs"] = xs
        m["ccol"] = np.ascontiguousarray(c[b].reshape(8, 128).T)
        m["pbias"] = pb
        maps.append(m)
    return maps


def gather_out(results):
    out = np.zeros((NB, SEQ, D), np.float32)
    for core in range(8):
        b, own, partner, vis = _core_layout(core)
        yc = np.asarray(results[core]["y"]).reshape(NSLOT, 512, D)
        ob = out[b].reshape(16, 512, D)
        for j in range(NSLOT):
            ob[own[j]] = yc[j]
    return out


def kernel(**inputs):
    if "nc" not in _NC_CACHE:
        _NC_CACHE["nc"] = build_nc()
    nc = _NC_CACHE["nc"]
    maps = make_in_maps(inputs)
    res = run_bass_kernel_spmd(nc, maps, core_ids=list(range(8)))
    return gather_out(res.results)
```

```python
import numpy as np
from contextlib import ExitStack
import concourse.bass as bass
import concourse.mybir as mybir
from concourse.bass_utils import run_bass_kernel_spmd

F32 = mybir.dt.float32
BF16 = mybir.dt.bfloat16
AF = mybir.ActivationFunctionType
ALU = mybir.AluOpType
AX = mybir.AxisListType

D = 1024
SEQ = 8192
NB = 4
DIN = 2560
DFF = 2816
NSLOT = 8
ALPHA = float(2.0 ** 0.25)
LAM_INIT = 0.2
LN_EPS = 1e-5
RMS_EPS = 1e-5
NEG = -30000.0
ENG = ("pe", "act", "dve", "pool", "sp")


class Sched:
    def __init__(self, nc, stack):
        self.nc = nc
        self.stack = stack
        self.ops = {e: [] for e in ENG}
        self.sem = {e: stack.enter_context(nc.semaphore("prog_" + e)) for e in ENG if e != "sp"}
        self.cnt = {e: 0 for e in ENG}
        self.seen = {e: {} for e in ENG}
        self.last_w = {}
        self.readers = {}
        self.dma_sems = {}
        self.self_sync = True

    def _deps(self, eng, reads, writes):
        need = {}

        def add(t):
            s, v, e = t
            if e == eng and (eng == "pe" or not self.self_sync):
                return
            k = id(s)
            if k not in need or need[k][1] < v:
                need[k] = (s, v)

        for b in reads:
            if b in self.last_w:
                add(self.last_w[b])
        for b in writes:
            if b in self.last_w:
                add(self.last_w[b])
            for t in self.readers.get(b, ()):
                add(t)
        out = []
        seen = self.seen[eng]
        for k, (s, v) in need.items():
            if seen.get(k, 0) >= v:
                continue
            seen[k] = v
            out.append((s, v))
        return out

    def _commit(self, tok, reads, writes):
        for b in writes:
            self.last_w[b] = tok
            self.readers[b] = []
        for b in reads:
            self.readers.setdefault(b, []).append(tok)

    def op(self, eng, fn, reads=(), writes=()):
        waits = self._deps(eng, reads, writes)
        self.cnt[eng] += 1
        tok = (self.sem[eng], self.cnt[eng], eng)
        self.ops[eng].append((waits, fn, self.sem[eng], 1))
        self._commit(tok, reads, writes)

    def dma(self, q, key, fn, reads=(), writes=(), n=1):
        if key not in self.dma_sems:
            self.dma_sems[key] = [self.stack.enter_context(self.nc.semaphore("dma_%d" % len(self.dma_sems))), 0]
        ent = self.dma_sems[key]
        waits = self._deps(q, reads, writes)
        ent[1] += 16 * n
        tok = (ent[0], ent[1], "dma")
        self.ops[q].append((waits, fn, ent[0], 16))
        self._commit(tok, reads, writes)

    def barrier(self):
        for e in ENG:
            waits = []
            seen = self.seen[e]
            for o in self.sem:
                if o == e:
                    continue
                s, v = self.sem[o], self.cnt[o]
                if v > seen.get(id(s), 0):
                    seen[id(s)] = v
                    waits.append((s, v))
            for k, (s, v) in self.dma_sems.items():
                if v > seen.get(id(s), 0):
                    seen[id(s)] = v
                    waits.append((s, v))
            if waits:
                self.ops[e].append((waits, None, None, 0))
        self.last_w = {}
        self.readers = {}

    def final_wait(self, eng, keys):
        waits = [(self.dma_sems[k][0], self.dma_sems[k][1]) for k in keys]
        self.ops[eng].append((waits, None, None, 0))

    def emit(self, block):
        engs = {"pe": block.tensor, "act": block.scalar, "dve": block.vector,
                "pool": block.gpsimd, "sp": block.sync}
        for e in ENG:
            ops = self.ops[e]

            def body(engine, ops=ops):
                for waits, fn, sem, inc in ops:
                    for s, v in waits:
                        engine.wait_ge(s, v)
                    if fn is None:
                        continue
                    r = fn(engine)
                    if isinstance(r, (list, tuple)):
                        for i in r:
                            i.then_inc(sem, inc)
                    else:
                        r.then_inc(sem, inc)

            engs[e](body)


def build_nc(debug=None, nslot_a=NSLOT, nslot_b=NSLOT):
    nc = bass.Bass("TRN2", target_bir_lowering=False)
    din = lambda name, shape: nc.dram_tensor(name, shape, F32, kind="ExternalInput").ap()
    xs = din("xs", [SEQ, D])
    ccol = din("ccol", [128, 8])
    pbias_d = din("pbias", [128, 8])
    w_ada = din("w_ada", [D, 6 * D])
    b_ada = din("b_ada", [1, 6 * D])
    w_in = din("w_in", [D, DIN])
    lam4 = din("lam4", [4, 64])
    subln_g = din("subln_g", [1, 128])
    sgu_g = din("sgu_ln_g", [1, 512])
    sgu_b = din("sgu_ln_b", [1, 512])
    w_sp = din("w_spatial", [4, 128, 128])
    b_sp = din("b_spatial", [4, 128])
    w_out = din("w_out", [D, D])
    ln1_g = din("ln1_g", [1, D])
    ln1_b = din("ln1_b", [1, D])
    w_gate = din("w_gate", [D, DFF])
    w_up = din("w_up", [D, DFF])
    w_down = din("w_down", [DFF, D])
    ln2_g = din("ln2_g", [1, D])
    ln2_b = din("ln2_b", [1, D])
    y = nc.dram_tensor("y", [NSLOT * 512, D], F32, kind="ExternalOutput").ap()
    dbg = None
    if debug:
        dbg = nc.dram_tensor("dbg", [128, 8 * 512], F32, kind="ExternalOutput").ap()
    scr = lambda name, shape: nc.dram_tensor(name, shape, BF16, kind="Internal").ap()
    wi_b = scr("wi_b", [D, DIN])
    wo_b = scr("wo_b", [D, D])
    wg_b = scr("wg_b", [D, DFF])
    wu_b = scr("wu_b", [D, DFF])
    wd_b = scr("wd_b", [DFF, D])

    with ExitStack() as st:
        T = lambda name, shape, dt: st.enter_context(nc.sbuf_tensor(name, shape, dt))
        S = Sched(nc, st)
        KT = T("KT", [128, 4, SEQ], BF16)
        arena = T("arena", [128, 33792], BF16)
        Vt = arena[:, 0:33280].rearrange("p (k h e) -> p k h e", k=64, h=4, e=130)
        wraw = [T("wraw%d" % i, [128, 2048], F32) for i in range(2)]
        xt_all = T("xt_all", [128, 4 * D], F32)
        xt = [xt_all[:, i * D:(i + 1) * D] for i in range(4)]
        wux = xt_all[:, 0:2048].bitcast(BF16).rearrange("p (k n) -> p k n", k=8)
        wqx = xt_all[:, 2048:4096].bitcast(BF16).rearrange("p (k n) -> p k n", k=8)
        hT = T("hT", [128, 8, 512], BF16)
        scrA = T("scrA", [128, 6656], F32)
        ident = T("ident", [128, 128], F32)
        identb = T("identb", [128, 128], BF16)
        triB = T("triB", [128, 128], BF16)
        wsT = T("wsT", [128, 4, 128], BF16)
        bsb = T("bsb", [128, 4, 128], F32)
        sgg = T("sgg", [128, 512], F32)
        sgb = T("sgb", [128, 512], F32)
        subg = T("subg", [128, 128], F32)
        ones_t = T("ones_t", [128, 128], F32)
        cols = T("cols", [128, 6, 8], F32)
        pbias = T("pbias_t", [128, 8], F32)
        small = T("small", [128, 64], F32)
        stats_all = T("stats", [128, 4, 24], F32)
        mv_all = T("mv", [128, 4, 8], F32)
        lnsm = T("lnsm", [128, 4, 8], F32)
        pst = st.enter_context(nc.psum_tensor("pst", [128, 8 * 512], F32))
        ps = [pst[:, i * 512:(i + 1) * 512] for i in range(8)]
        psb = [p.bitcast(BF16) for p in ps]
        S2 = [pst[:, 512:1536].rearrange("p (m n) -> p m n", m=2), pst[:, 1536:2560].rearrange("p (m n) -> p m n", m=2)]
        S2K = [["B1", "B2"], ["B3", "B4"]]

        QTall = scrA[:, 0:1024].bitcast(BF16)
        QTall2 = wraw[0][:, 0:1024].bitcast(BF16)
        QTp = [scrA[:, 512 * i:512 * (i + 1)].bitcast(BF16).rearrange("p (m n) -> p m n", m=2) for i in range(2)] + \
              [wraw[0][:, 512 * i:512 * (i + 1)].bitcast(BF16).rearrange("p (m n) -> p m n", m=2) for i in range(2)]
        uT = scrA[:, 1024:3072].rearrange("p (g n) -> p g n", g=4)
        PT2 = [hT[:, 2 * i:2 * i + 2, :] for i in range(3)]
        atok = [scrA[:, 2560 + 256 * r:2560 + 256 * (r + 1)].bitcast(BF16).rearrange("p (t e) -> p t e", t=4)
                for r in range(2)]
        gv = [scrA[:, 3072:3584], scrA[:, 5632:6144], scrA[:, 6144:6656], scrA[:, 4608:5120]]
        vn = scrA[:, 3584:4608].bitcast(BF16).rearrange("p (t n) -> p t n", t=4)
        goT = scrA[:, 4608:5632].bitcast(BF16).rearrange("p (g n) -> p g n", g=4)
        Ocp = scrA[:, 3072:3072 + 1032]
        Ocp3 = Ocp.rearrange("p (a e) -> p a e", a=8)
        t2b = scrA[:, 6144:6656].rearrange("p (t e) -> p t e", t=4)
        ddall = scrA[:, 5632:6144].rearrange("p (t e) -> p t e", t=4)
        wb_raw = [wraw[0][:, :], wraw[1][:, :], scrA[:, 0:2048], scrA[:, 2048:4096], scrA[:, 4096:6144]]
        wbuf = [r.bitcast(BF16).rearrange("p (k n) -> p k n", k=8) for r in wb_raw]
        wbuf22 = [r.bitcast(BF16) for r in wb_raw]
        wf32 = [r.rearrange("p (k n) -> p k n", k=8) for r in wb_raw]
        x1 = arena[:, 0:8192].bitcast(F32).rearrange("p (t d) -> p t d", t=4)
        actT = arena[:, 8192:8192 + 22 * 512].rearrange("p (f n) -> p f n", f=22)
        cb = 8192 + 22 * 512
        L1G, L1B, L2G, L2B = [arena[:, cb + 2048 * i:cb + 2048 * (i + 1)].bitcast(F32) for i in range(4)]
        hT2 = arena[:, cb + 8192:cb + 12288].rearrange("p (k n) -> p k n", k=8)
        wbuf.append(hT2)
        sgbuf2 = arena[:, cb + 12288:cb + 14336].bitcast(F32)
        assert cb + 14336 <= 33792, cb
        wsin = [scrA[:, 0:1024]]
        wsout = [scrA[:, 1024:1536].bitcast(BF16)]
        Grow = [scrA[:, 1536:2560], scrA[:, 2560:3584]]
        wch = [scrA[:, 3584:4608], scrA[:, 4608:5632]]
        acc = scrA[:, 5632:6656]
        zcol = small[:, 0:1]
        lamc = small[:, 1:2]
        rr = small[:, 2:6]
        nhalf = small[:, 52:56]
        rr8 = small[:, 56:64]
        ss4 = small[:, 44:48]
        rs4 = small[:, 48:52]
        rstd4 = small[:, 8:12]
        nmr4 = small[:, 12:16]
        lnst = small[:, 16:28]
        lnmv = small[:, 28:30]
        lnr = small[:, 30:32]
        cact = small[:, 32:40]
        lsum = small[:, 40:44]

        bank_rot = {"i": 0}

        def KTkey(h, pos):
            return ("KT", h, pos)

        def cast_w(key, dst, src, nchunk):
            S.dma("pool", key,
                  lambda e: [e.dma_start(out=dst[k * 128:(k + 1) * 128, :], in_=src[k * 128:(k + 1) * 128, :])
                             for k in range(nchunk)], writes=[key], n=nchunk)

        def mk_a(e):
            e.memset(ident[:, :], 0.0)
            e.memset(small[:, 0:52], 0.0)
            e.memset(small[:, 52:56], -0.5)
            e.memset(small[:, 56:64], 0.0)
            e.memset(triB[:, :], 0.0)
            e.memset(ones_t[:, :], 1.0)
            return e.memset(arena[:, 0:33280].rearrange("p (kh e) -> p kh e", e=130)[:, :, 128:129], 1.0)
        S.op("pool", mk_a, writes=["ident", "small", "triB", "ones_t", "Vones", "cact0"])
        S.op("pool", lambda e: e.affine_select(out=ident[:, :], in_=ident[:, :], pattern=[[-1, 128]], compare_op=ALU.not_equal,
                                               fill=1.0, base=0, channel_multiplier=1), reads=["ident"], writes=["ident"])
        S.op("pool", lambda e: e.affine_select(out=triB[:, :], in_=triB[:, :], pattern=[[1, 128]], compare_op=ALU.is_ge,
                                               fill=NEG, base=0, channel_multiplier=-1), reads=["triB"], writes=["triB"])
        def cast_wi(g_):
            S.dma("pool", "wi_b%d" % g_,
                  lambda e, g_=g_: [e.dma_start(out=wi_b[k * 128:(k + 1) * 128, g_ * 512:(g_ + 1) * 512],
                                                in_=w_in[k * 128:(k + 1) * 128, g_ * 512:(g_ + 1) * 512]) for k in range(8)],
                  writes=[("wi_b", g_)], n=8)
        for g_ in (1, 2):
            cast_wi(g_)
        for g_ in ():
            S.dma("pool", "wi_b%d" % g_,
                  lambda e, g_=g_: [e.dma_start(out=wi_b[k * 128:(k + 1) * 128, g_ * 512:(g_ + 1) * 512],
                                                in_=w_in[k * 128:(k + 1) * 128, g_ * 512:(g_ + 1) * 512]) for k in range(8)],
                  writes=[("wi_b", g_)], n=8)
        S.dma("sp", "ccol", lambda e: e.dma_start(out=cact, in_=ccol), reads=["small"], writes=["cact"])
        S.dma("sp", "pbias", lambda e: e.dma_start(out=pbias[:, :], in_=pbias_d), writes=["pbias"])
        lamt = xt[3][:, 0:256]
        S.dma("sp", "lam4", lambda e: e.dma_start(out=lamt, in_=lam4.rearrange("a b -> (a b)").partition_broadcast(128)),
              writes=["lamt"])
        S.dma("sp", "subg", lambda e: e.dma_start(out=subg[:, :], in_=subln_g.rearrange("a b -> (a b)").partition_broadcast(128)),
              writes=["subg"])
        S.dma("sp", "sgg", lambda e: e.dma_start(out=sgg[:, :], in_=sgu_g.rearrange("a b -> (a b)").partition_broadcast(128)),
              writes=["sgg"])
        S.dma("sp", "sgb", lambda e: e.dma_start(out=sgb[:, :], in_=sgu_b.rearrange("a b -> (a b)").partition_broadcast(128)),
              writes=["sgb"])
        S.dma("sp", "bsb", lambda e: e.dma_start(out=bsb[:, :, :].rearrange("p g t -> p (g t)"),
                                                 in_=b_sp.rearrange("a b -> (a b)").partition_broadcast(128)),
              writes=["bsb"])
        wspt = xt[2][:, 0:512].rearrange("p (g s) -> p g s", g=4)
        S.dma("sp", "wsp", lambda e: e.dma_start(out=wspt, in_=w_sp.rearrange("g t s -> t g s")), writes=["wspt"])

        S.op("dve", lambda e: e.tensor_copy(out=identb[:, :], in_=ident[:, :]), reads=["ident"], writes=["identb"])
        def mask_ws(e):
            r = None
            for g in range(4):
                r = e.affine_select(out=wspt[:, g, :], in_=wspt[:, g, :], pattern=[[-1, 128]], compare_op=ALU.is_ge,
                                    fill=0.0, base=0, channel_multiplier=1)
            return r
        S.op("pool", mask_ws, reads=["wspt"], writes=["wspt"])
        def tr_ws(e):
            r = None
            for g in range(4):
                r = e.transpose(out=ps[0][:, g * 128:(g + 1) * 128], in_=wspt[:, g, :], identity=ident[:, :])
            return r
        S.op("pe", tr_ws, reads=["wspt", "ident"], writes=["B0"])
        S.op("dve", lambda e: e.tensor_copy(out=wsT[:, :, :].rearrange("p g t -> p (g t)"), in_=ps[0][:, :]),
             reads=["B0"], writes=["wsT"])
        S.op("dve", lambda e: e.tensor_scalar(out=subg[:, :], in0=subg[:, :], scalar1=float((1.0 - LAM_INIT) * np.sqrt(128.0)),
                                              scalar2=None, op0=ALU.mult), reads=["subg"], writes=["subg"])
        lam3 = lamt.rearrange("p (a b) -> p a b", a=4)
        def lam_a(e):
            e.tensor_tensor(out=lam3[:, 0, :], in0=lam3[:, 0, :], in1=lam3[:, 1, :], op=ALU.mult)
            return e.tensor_tensor(out=lam3[:, 2, :], in0=lam3[:, 2, :], in1=lam3[:, 3, :], op=ALU.mult)
        S.op("dve", lam_a, reads=["lamt"], writes=["lamt"])
        def lam_b(e):
            e.reduce_sum(out=lsum[:, 0:1], in_=lam3[:, 0, :], axis=AX.X)
            return e.reduce_sum(out=lsum[:, 1:2], in_=lam3[:, 2, :], axis=AX.X)
        S.op("dve", lam_b, reads=["lamt", "small"], writes=["lsum"])
        S.op("act", lambda e: e.activation(out=lsum[:, 2:4], in_=lsum[:, 0:2], func=AF.Exp), reads=["lsum"], writes=["lsum2"])
        S.op("dve", lambda e: e.scalar_tensor_tensor(out=lamc, in0=lsum[:, 2:3], scalar=LAM_INIT, in1=lsum[:, 3:4],
                                                     op0=ALU.add, op1=ALU.subtract), reads=["lsum2"], writes=["lamc"])
        S.op("act", lambda e: e.activation(out=cact, in_=cact, func=AF.Silu), reads=["cact", "cact0"], writes=["cact"])
        modrow = acc

        wchS = [wch[0], wch[1], scrA[:, 0:1024], scrA[:, 1536:2560]]

        def mod_dma(part, kc, q="sp"):
            b = kc % (4 if q == "sp" else 2)
            S.dma(q, ("wch%d" if q == "sp" else "pwch%d") % b,
                  lambda e: e.dma_start(out=wchS[b], in_=w_ada[kc * 128:(kc + 1) * 128, part * 1024:(part + 1) * 1024]),
                  writes=["wch%d" % b])

        def mod_step(part, kc, q="sp", dma=True):
            b = kc % (4 if q == "sp" else 2)
            if dma:
                mod_dma(part, kc, q)
            if kc == 0:
                S.op("dve", lambda e: e.tensor_scalar(out=acc, in0=wchS[b], scalar1=cact[:, kc:kc + 1], scalar2=None, op0=ALU.mult),
                     reads=["wch%d" % b, "cact"], writes=["acc"])
            else:
                S.op("dve", lambda e: e.scalar_tensor_tensor(out=acc, in0=wchS[b], scalar=cact[:, kc:kc + 1], in1=acc,
                                                             op0=ALU.mult, op1=ALU.add),
                     reads=["wch%d" % b, "cact", "acc"], writes=["acc"])

        def mod_bias_dma(part, q="sp"):
            S.dma(q, "wch0" if q == "sp" else "pwch0", lambda e: e.dma_start(out=wch[0][0:1, :], in_=b_ada[0:1, part * 1024:(part + 1) * 1024]),
                  writes=["wch0"])

        def mod_finish(part, banks, q="sp", dma=True):
            if dma:
                mod_bias_dma(part, q)
            S.op("dve", lambda e: e.tensor_tensor(out=acc[0:1, :], in0=acc[0:1, :], in1=wch[0][0:1, :], op=ALU.add),
                 reads=["acc", "wch0"], writes=["acc"])
            addc = 0.0 if part in (0, 3) else 1.0
            for half in range(2):
                bk = banks[half]
                S.op("pe", lambda e, half=half, bk=bk: e.matmul(out=ps[bk][:, :], lhsT=ones_t[:, :],
                                                               rhs=acc[:, half * 512:(half + 1) * 512], start=True, stop=True),
                     reads=["acc", "ones_t"], writes=["B%d" % bk])
            for half in range(2):
                bk = banks[half]
                S.op("dve", lambda e, half=half, bk=bk: e.tensor_scalar(out=modrow[:, half * 512:(half + 1) * 512], in0=ps[bk][:, :],
                                                                        scalar1=addc, scalar2=None, op0=ALU.add),
                     reads=["B%d" % bk], writes=["acc"])
            if part in (2, 5):
                S.op("dve", lambda e, gi=part // 3: e.tensor_copy(out=Grow[gi], in_=modrow), reads=["acc"], writes=["Grow%d" % (part // 3)])
            for half in range(2):
                bk = banks[half]
                def trm(e, half=half, bk=bk):
                    r = None
                    for k in range(4):
                        kc = half * 4 + k
                        r = e.transpose(out=ps[bk][:, k * 128:(k + 1) * 128], in_=modrow[:, kc * 128:(kc + 1) * 128],
                                        identity=ident[:, :])
                    return r
                S.op("pe", trm, reads=["acc", "ident"], writes=["B%d" % bk])
                S.op("dve", lambda e, half=half, bk=bk, part=part: e.tensor_copy(
                    out=cols[:, part, half * 4:half * 4 + 4],
                    in_=ps[bk][:, :].rearrange("p (k c) -> p k c", c=128)[:, :, 0]),
                    reads=["B%d" % bk], writes=["cols"])

        for part in (0, 1, 5):
            for kc in range(8):
                mod_step(part, kc)
            mod_finish(part, [0, 1])
        sh1c, sc1c, g1c, sh2c, sc2c, g2c = [cols[:, i, :] for i in range(6)]
        S.barrier()

        wkeys = {id(wi_b): "wi_b", id(wo_b): "wo_b", id(wg_b): "wg_b", id(wu_b): "wu_b", id(wd_b): "wd_b"}

        def load_w(i, src, k0, nk, n0, ncols, q="sp"):
            key = "wbuf%d" % i
            srcv = src.rearrange("(k p) n -> p k n", p=128)[:, k0:k0 + nk, n0:n0 + ncols]
            wk = wkeys[id(src)]
            if wk in ("wo_b", "wd_b"):
                rd = [(wk, kc) for kc in range(k0, k0 + nk)]
            elif wk == "wi_b":
                rd = [(wk, n0 // 512)]
            else:
                rd = [wk]
            S.dma(q, key, lambda e: e.dma_start(out=wbuf[i][:, 0:nk, 0:ncols], in_=srcv), reads=rd, writes=[key])

        def load_x_tile(pos, q="sp"):
            for tb in range(4):
                r0 = pos * 512 + tb * 128
                S.dma(q, "xt%d" % tb, lambda e, tb=tb, r0=r0: e.dma_start(out=xt[tb][:, :], in_=xs[r0:r0 + 128, :]),
                      writes=["xt%d" % tb])

        def next_bank(lst):
            k = tuple(lst)
            i = bank_rot.get(k, 0)
            bank_rot[k] = i + 1
            return lst[i % len(lst)]

        evac_rot = {"i": 0}

        def transpose_mod(srcs, srckeys, scc, shc, banks, engines, dst=None, dk="hT", kcs=range(8)):
            if dst is None:
                dst = hT
            for kc in kcs:
                bk = next_bank(banks)
                def tr(e, kc=kc, bk=bk):
                    r = None
                    for tb in range(4):
                        r = e.transpose(out=ps[bk][:, tb * 128:(tb + 1) * 128], in_=srcs[tb][:, kc * 128:(kc + 1) * 128],
                                        identity=ident[:, :])
                    return r
                S.op("pe", tr, reads=list(srckeys) + ["ident"], writes=["B%d" % bk])
                eng = engines[evac_rot["i"] % len(engines)]
                evac_rot["i"] += 1
                if eng == "act":
                    S.op("act", lambda e, kc=kc, bk=bk: e.activation(out=dst[:, kc, :], in_=ps[bk][:, :], func=AF.Identity,
                                                                     bias=shc[:, kc:kc + 1], scale=scc[:, kc:kc + 1]),
                         reads=["B%d" % bk, "cols"], writes=[(dk, kc)])
                else:
                    S.op("dve", lambda e, kc=kc, bk=bk: e.tensor_scalar(out=dst[:, kc, :], in0=ps[bk][:, :],
                                                                        scalar1=scc[:, kc:kc + 1], scalar2=shc[:, kc:kc + 1],
                                                                        op0=ALU.mult, op1=ALU.add),
                         reads=["B%d" % bk, "cols"], writes=[(dk, kc)])

        hTkeys = [("hT", kc) for kc in range(8)]

        def proj_fm(wi, c, bk, src=None, sk="hT"):
            if src is None:
                src = hT
            def fn(e):
                r = None
                for kc in range(8):
                    r = e.matmul(out=ps[bk][:, :], lhsT=wbuf[wi][:, kc, c * 128:(c + 1) * 128], rhs=src[:, kc, :],
                                 start=(kc == 0), stop=(kc == 7))
                return r
            S.op("pe", fn, reads=[(sk, kc) for kc in range(8)] + ["wbuf%d" % wi], writes=["B%d" % bk])

        def proj_tm(wi, tb, bk):
            def fn(e):
                r = None
                for kc in range(8):
                    r = e.matmul(out=ps[bk][:, :], lhsT=hT[:, kc, tb * 128:(tb + 1) * 128], rhs=wbuf[wi][:, kc, :],
                                 start=(kc == 0), stop=(kc == 7))
                return r
            S.op("pe", fn, reads=hTkeys + ["wbuf%d" % wi], writes=["B%d" % bk])

        xkeys = ["xt%d" % i for i in range(4)]

        load_w(0, wi_b, 0, 8, 512, 512)
        load_w(1, wi_b, 0, 8, 1024, 512)
        P0_TP = [0, 1, 2]
        P0_PJ = [7, 3, 4, 5, 6]
        npos0 = 2 * nslot_a
        def scale_chunk(idx):
            gi, kc = (1, idx) if idx < 22 else (0, idx - 22)
            srcw, dstw, name = ((w_out, wo_b, "wo_b"), (w_down, wd_b, "wd_b"))[gi]
            b = 0
            S.dma("pool", "wsin%d" % b, lambda e: e.dma_start(out=wsin[b], in_=srcw[kc * 128:(kc + 1) * 128, :]), writes=["wsin%d" % b])
            S.op("pool", lambda e: e.tensor_tensor(out=wsout[b], in0=wsin[b], in1=Grow[gi], op=ALU.mult),
                 reads=["wsin%d" % b, "Grow%d" % gi], writes=["wsout%d" % b])
            S.dma("pool", "wsout%d" % b, lambda e: e.dma_start(out=dstw[kc * 128:(kc + 1) * 128, :], in_=wsout[b]),
                  reads=["wsout%d" % b], writes=[(name, kc)])
        sc_i = 0
        md_i = 0
        for pos in range(npos0):
            load_x_tile(pos)
            if pos in (1, 3, 5):
                cast_wi({1: 0, 3: 4, 5: 3}[pos])
            for _ in range(2):
                if sc_i < 30:
                    scale_chunk(sc_i)
                    sc_i += 1
            k_, r_ = pos // 5, pos % 5
            fin_part = (1 + k_) if (r_ == 0 and 1 <= k_ <= 3) else None
            if k_ < 3:
                part_ = 2 + k_
                if r_ >= 1:
                    for kc_ in (2 * r_ - 2, 2 * r_ - 1):
                        mod_step(part_, kc_, q="pool", dma=False)
                if r_ == 0 and fin_part is not None:
                    pass
                elif r_ <= 3:
                    for kc_ in (2 * r_, 2 * r_ + 1):
                        mod_dma(part_, kc_, q="pool")
                else:
                    mod_bias_dma(part_, q="pool")
            transpose_mod([x[:, :] for x in xt], xkeys, sc1c, sh1c, P0_TP, ["act"], kcs=range(0, 4))
            for h in range(4):
                def k1(e, h=h):
                    r = None
                    for kc in range(4):
                        r = e.matmul(out=ps[3 + h][:, :], lhsT=wbuf[0][:, kc, h * 128:(h + 1) * 128], rhs=hT[:, kc, :],
                                     start=(kc == 0), stop=False)
                    return r
                S.op("pe", k1, reads=[("hT", kc) for kc in range(4)] + ["wbuf0"], writes=["B%d" % (3 + h)])
            if fin_part is not None:
                mod_finish(fin_part, [7, 2], q="pool", dma=False)
                if k_ < 3:
                    for kc_ in (0, 1):
                        mod_dma(2 + k_, kc_, q="pool")
            transpose_mod([x[:, :] for x in xt], xkeys, sc1c, sh1c, P0_TP, ["act"], kcs=range(4, 8))
            for h in range(4):
                bk = 3 + h
                def k2(e, h=h, bk=bk):
                    r = None
                    for kc in range(4, 8):
                        r = e.matmul(out=ps[bk][:, :], lhsT=wbuf[0][:, kc, h * 128:(h + 1) * 128], rhs=hT[:, kc, :],
                                     start=False, stop=(kc == 7))
                    return r
                S.op("pe", k2, reads=[("hT", kc) for kc in range(4, 8)] + ["wbuf0"], writes=["B%d" % bk])
                dst = KT[:, h, pos * 512:(pos + 1) * 512]
                if h % 2 == 0:
                    S.op("act", lambda e, dst=dst, bk=bk: e.activation(out=dst, in_=ps[bk][:, :], func=AF.Identity),
                         reads=["B%d" % bk], writes=[KTkey(h, pos)])
                else:
                    S.op("dve", lambda e, dst=dst, bk=bk: e.tensor_copy(out=dst, in_=ps[bk][:, :]),
                         reads=["B%d" % bk], writes=[KTkey(h, pos)])
            for tb in range(4):
                bk = next_bank(P0_PJ)
                proj_tm(1, tb, bk)
                kb = pos * 4 + tb
                dst = Vt[:, kb, :, 0:128]
                src = ps[bk][:, :].rearrange("p (h e) -> p h e", h=4)
                if tb % 2 == 0:
                    S.op("act", lambda e, dst=dst, src=src: e.activation(out=dst, in_=src, func=AF.Identity),
                         reads=["B%d" % bk], writes=[("V", kb)])
                else:
                    S.op("dve", lambda e, dst=dst, src=src: e.tensor_copy(out=dst, in_=src),
                         reads=["B%d" % bk], writes=[("V", kb)])
        while sc_i < 30:
            scale_chunk(sc_i)
            sc_i += 1
        S.barrier()

        GEN = [1, 2, 3, 4]
        def Oacc(m, tb):
            a = m * 4 + tb
            return ps[5 + a // 3][:, (a % 3) * 129:(a % 3) * 129 + 129], ("O", a)
        Ukey = lambda i: ("U", i)
        GVK = [[("gv", 0)], [("gv", 1)] + [("dd", t) for t in range(4)], ["t2b"], [("goT", 0), ("goT", 1)]]

        def mixT_view(j, c):
            return KT[:, c // 2, 1024 * j + (c % 2) * 512:1024 * j + (c % 2) * 512 + 512], KTkey(c // 2, 2 * j + (c % 2))

        deferred_casts = [(key, dst, src, k) for k in range(8) for (key, dst, src) in (("wg_b", wg_b, w_gate), ("wu_b", wu_b, w_up))]

        def issue_cast():
            if deferred_casts:
                key, dst, src, k = deferred_casts.pop(0)
                S.dma("pool", key, lambda e: e.dma_start(out=dst[k * 128:(k + 1) * 128, :], in_=src[k * 128:(k + 1) * 128, :]),
                      writes=[key])
        load_w(1, wi_b, 0, 8, 2048, 512)
        def zq(e):
            e.memset(QTall, 0.0)
            return e.memset(QTall2, 0.0)
        S.op("pool", zq, writes=[("QT", h) for h in range(4)])

        def qproj(h, banks, eng="dve"):
            bk = next_bank(banks)
            def fnq(e, h=h, bk=bk):
                r = None
                for kc in range(8):
                    r = e.matmul(out=ps[bk][:, :], lhsT=wqx[:, kc, h * 128:(h + 1) * 128], rhs=hT[:, kc, :],
                                 start=(kc == 0), stop=(kc == 7))
                return r
            S.op("pe", fnq, reads=hTkeys + ["xt2", "xt3"], writes=["B%d" % bk])
            hb = h
            if eng == "act":
                def cp(e, bk=bk, hb=hb):
                    e.activation(out=QTp[hb][0:64, 0, :], in_=ps[bk][0:64, :], func=AF.Identity)
                    return e.activation(out=QTp[hb][64:128, 1, :], in_=ps[bk][64:128, :], func=AF.Identity)
            else:
                def cp(e, bk=bk, hb=hb):
                    e.tensor_copy(out=QTp[hb][0:64, 0, :], in_=ps[bk][0:64, :])
                    return e.tensor_copy(out=QTp[hb][64:128, 1, :], in_=ps[bk][64:128, :])
            S.op(eng, cp, reads=["B%d" % bk], writes=[("QT", hb)])

        def slot_pre(j, evac):
            transpose_mod([x[:, :] for x in xt], xkeys, sc1c, sh1c, GEN, evac)
            for tb in range(4):
                bk = next_bank(GEN)
                proj_tm(1, tb, bk)
                gvb = gv[tb]
                gk = GVK[tb]
                stats = stats_all[:, tb, :].rearrange("p (g s) -> p g s", g=4)
                mv = mv_all[:, tb, :].rearrange("p (g s) -> p g s", g=4)
                rstd4 = lnsm[:, tb, 0:4]
                nmr4 = lnsm[:, tb, 4:8]
                sk_, mk2_, rk_, nk_ = ("stats", tb), ("mv", tb), ("rstd4", tb), ("nmr4", tb)
                S.op("act", lambda e, gvb=gvb, bk=bk: e.activation(out=gvb, in_=ps[bk][:, :], func=AF.Gelu_apprx_tanh),
                     reads=["B%d" % bk], writes=gk)
                def st_(e, gvb=gvb, stats=stats):
                    r = None
                    for g in range(4):
                        r = e.bn_stats(out=stats[:, g, :], in_=gvb[:, g * 128:(g + 1) * 128])
                    return r
                S.op("dve", st_, reads=gk, writes=[sk_])
                def ag_(e, stats=stats, mv=mv):
                    r = None
                    for g in range(4):
                        r = e.bn_aggr(out=mv[:, g, :], in_=stats[:, g, :])
                    return r
                S.op("dve", ag_, reads=[sk_], writes=[mk2_])
                S.op("dve", lambda e, mv=mv, rstd4=rstd4: e.tensor_scalar(out=rstd4, in0=mv[:, :, 1], scalar1=LN_EPS, scalar2=None,
                                                                         op0=ALU.add), reads=[mk2_], writes=[rk_])
                S.op("pool", lambda e, rstd4=rstd4: e.tensor_tensor(out=rstd4, in0=rstd4, in1=nhalf, op=ALU.pow),
                     reads=[rk_], writes=[rk_])
                S.op("dve", lambda e, mv=mv, rstd4=rstd4, nmr4=nmr4: e.scalar_tensor_tensor(out=nmr4, in0=mv[:, :, 0], scalar=-1.0, in1=rstd4,
                                                                                        op0=ALU.mult, op1=ALU.mult),
                     reads=[mk2_, rk_], writes=[nk_])
                def nrm_(e, gvb=gvb, rstd4=rstd4, nmr4=nmr4):
                    r = None
                    for g in range(4):
                        r = e.tensor_scalar(out=gvb[:, g * 128:(g + 1) * 128], in0=gvb[:, g * 128:(g + 1) * 128],
                                            scalar1=rstd4[:, g:g + 1], scalar2=nmr4[:, g:g + 1], op0=ALU.mult, op1=ALU.add)
                    return r
                S.op("dve", nrm_, reads=gk + [rk_, nk_], writes=gk)
                S.op("pool", lambda e, gvb=gvb: e.tensor_tensor(out=gvb, in0=gvb, in1=sgg[:, :], op=ALU.mult),
                     reads=gk + ["sgg"], writes=gk)
                S.op("pool", lambda e, gvb=gvb, tb=tb: e.tensor_tensor(out=vn[:, tb, :], in0=gvb, in1=sgb[:, :], op=ALU.add),
                     reads=gk + ["sgb"], writes=[("vn", tb)])

            S.dma("sp", "wux", lambda e: e.dma_start(out=wux[:, :, :], in_=wi_b.rearrange("(k p) n -> p k n", p=128)[:, :, 1536:2048]),
                  reads=[("wi_b", 3)], writes=["xt0", "xt1"])
            S.dma("sp", "wqx", lambda e: e.dma_start(out=wqx[:, :, :], in_=wi_b.rearrange("(k p) n -> p k n", p=128)[:, :, 0:512]),
                  reads=[("wi_b", 0)], writes=["xt2", "xt3"])

        def slot_mid(j):
            for g in range(4):
                bk = next_bank(GEN)
                def fn(e, g=g, bk=bk):
                    r = None
                    for kc in range(8):
                        r = e.matmul(out=ps[bk][:, :], lhsT=wux[:, kc, g * 128:(g + 1) * 128], rhs=hT[:, kc, :],
                                     start=(kc == 0), stop=(kc == 7))
                    return r
                S.op("pe", fn, reads=hTkeys + ["xt0", "xt1"], writes=["B%d" % bk])
                S.op("act", lambda e, g=g, bk=bk: e.activation(out=uT[:, g, :], in_=ps[bk][:, :], func=AF.Gelu_apprx_tanh),
                     reads=["B%d" % bk], writes=[Ukey(2 * g), Ukey(2 * g + 1)])
            for h in range(4):
                qproj(h, GEN, eng="act")
            if j > 0:
                load_x_tile(2 * (j - 1))
            def sgu_group(g):
                bk = 0
                def sg_(e, g=g, bk=bk):
                    r = None
                    for tb in range(4):
                        r = e.matmul(out=ps[bk][:, tb * 128:(tb + 1) * 128], lhsT=vn[:, tb, g * 128:(g + 1) * 128],
                                     rhs=wsT[:, g, :], start=True, stop=True)
                    return r
                S.op("pe", sg_, reads=[("vn", t) for t in range(4)] + ["wsT"], writes=["B%d" % bk])
                gtmp = gv[g % 2]
                gk = GVK[g % 2]
                S.op("dve", lambda e, g=g, bk=bk, gtmp=gtmp: e.tensor_tensor(
                    out=gtmp.rearrange("p (t n) -> p t n", t=4), in0=ps[bk][:, :].rearrange("p (t n) -> p t n", t=4),
                    in1=bsb[:, g:g + 1, :].to_broadcast([128, 4, 128]), op=ALU.add),
                    reads=["B%d" % bk, "bsb"], writes=gk)
                S.op("dve", lambda e, g=g, gtmp=gtmp: e.tensor_tensor(out=goT[:, g, :], in0=gtmp, in1=uT[:, g, :], op=ALU.mult),
                     reads=gk + [Ukey(2 * g), Ukey(2 * g + 1)], writes=[("goT", g)])
            return [(lambda g=g: sgu_group(g)) for g in range(4)]

        def attention_head(j, h, hooks):
            nkb = 8 * (j + 1)
            hb = h
            def qk(i, h=h, j=j, hb=hb):
                kb = i
                d = kb - 8 * j
                diag = 0 <= d < 4
                c0 = 128 * d if diag else 0
                sb = i % 2
                kw = KT[:, h, kb * 128:(kb + 1) * 128]
                def fn(e):
                    r = None
                    for m in range(2):
                        if diag:
                            e.matmul(out=S2[sb][:, m, c0:c0 + 128], lhsT=kw, rhs=QTp[hb][:, m, c0:c0 + 128], start=True, stop=False)
                            r = e.matmul(out=S2[sb][:, m, c0:c0 + 128], lhsT=identb[:, :], rhs=triB[:, :], start=False, stop=True)
                            if c0 + 128 < 512:
                                r = e.matmul(out=S2[sb][:, m, c0 + 128:512], lhsT=kw, rhs=QTp[hb][:, m, c0 + 128:512],
                                             start=True, stop=True)
                        else:
                            r = e.matmul(out=S2[sb][:, m, :], lhsT=kw, rhs=QTp[hb][:, m, :], start=True, stop=True)
                    return r
                S.op("pe", fn, reads=[KTkey(h, kb // 4), ("QT", hb), "identb", "triB"], writes=S2K[sb])
                pti = i % 3
                bias = pbias[:, j:j + 1] if d >= 4 else zcol
                S.op("act", lambda e: e.activation(out=PT2[pti][:, :, c0:512], in_=S2[sb][:, :, c0:512], func=AF.Exp,
                                                   bias=bias, scale=0.125),
                     reads=S2K[sb] + ["pbias", "small"], writes=[("hT", 2 * pti), ("hT", 2 * pti + 1)])
            def pv(i, h=h, j=j, nkb=nkb):
                kb = i
                d = kb - 8 * j
                tb0 = d if 0 <= d < 4 else 0
                pti = i % 3
                def fn(e):
                    r = None
                    for m in range(2):
                        for tb in range(tb0, 4):
                            o, _ = Oacc(m, tb)
                            r = e.matmul(out=o, lhsT=PT2[pti][:, m, tb * 128:(tb + 1) * 128], rhs=Vt[:, kb, h, 0:129],
                                         start=(kb == 0 and (m * 4 + tb) % 3 == 0), stop=(kb == nkb - 1), skip_group_check=True)
                    return r
                S.op("pe", fn, reads=[("hT", 2 * pti), ("hT", 2 * pti + 1), ("V", kb)],
                     writes=[Oacc(m, tb)[1] for m in range(2) for tb in range(tb0, 4)])

            n = nkb
            qk(0)
            for i in range(n):
                if i + 1 < n:
                    qk(i + 1)
                pv(i)
                for hk_ in hooks.get(i, ()):
                    hk_()
            ar = h % 2
            OK_ = [("O", a_) for a_ in range(8)]
            OCK = [("gv", 0)] + [("vn", t) for t in range(4)]
            DDK = GVK[1]
            def ocp(e):
                e.tensor_copy(out=Ocp[:, 0:387], in_=ps[5][:, 0:387])
                e.tensor_copy(out=Ocp[:, 387:774], in_=ps[6][:, 0:387])
                return e.tensor_copy(out=Ocp[:, 774:1032], in_=ps[7][:, 0:258])
            S.op("dve", ocp, reads=OK_, writes=OCK)
            S.op("dve", lambda e: e.reciprocal(out=rr8, in_=Ocp3[:, :, 128]), reads=OCK, writes=["rr8"])
            S.op("dve", lambda e: e.tensor_scalar(out=rr, in0=rr8[:, 4:8], scalar1=lamc, scalar2=None, op0=ALU.mult),
                 reads=["rr8", "lamc"], writes=["rr"])
            S.op("dve", lambda e: e.tensor_tensor(out=t2b, in0=Ocp3[:, 4:8, 0:128],
                                                  in1=rr.unsqueeze(2).to_broadcast([128, 4, 128]), op=ALU.mult),
                 reads=OCK + ["rr"], writes=["t2b"])
            S.op("dve", lambda e: e.tensor_tensor(out=ddall, in0=Ocp3[:, 0:4, 0:128],
                                                  in1=rr8[:, 0:4].unsqueeze(2).to_broadcast([128, 4, 128]), op=ALU.mult),
                 reads=OCK + ["rr8"], writes=DDK)
            S.op("dve", lambda e: e.tensor_tensor(out=ddall, in0=ddall, in1=t2b, op=ALU.subtract),
                 reads=DDK + ["t2b"], writes=DDK)
            S.op("dve", lambda e: e.tensor_tensor(out=t2b, in0=ddall, in1=ddall, op=ALU.mult), reads=DDK, writes=["t2b"])
            S.op("dve", lambda e: e.reduce_sum(out=ss4, in_=t2b, axis=AX.X), reads=["t2b"], writes=["ss4"])
            S.op("dve", lambda e: e.tensor_scalar(out=rs4, in0=ss4, scalar1=float(128.0 * RMS_EPS), scalar2=None,
                                                  op0=ALU.add), reads=["ss4"], writes=["rs4"])
            S.op("pool", lambda e: e.tensor_tensor(out=rs4, in0=rs4, in1=nhalf, op=ALU.pow), reads=["rs4"], writes=["rs4"])
            S.op("dve", lambda e: e.tensor_tensor(out=ddall, in0=ddall, in1=rs4.unsqueeze(2).to_broadcast([128, 4, 128]),
                                                  op=ALU.mult), reads=DDK + ["rs4"], writes=DDK)
            S.op("dve", lambda e, ar=ar: e.tensor_tensor(out=atok[ar], in0=ddall,
                                                         in1=subg[:, :].unsqueeze(1).to_broadcast([128, 4, 128]), op=ALU.mult),
                 reads=DDK + ["subg"], writes=[Ukey(6 + ar)])

            def fin(h=h, ar=ar, j=j):
                bk = 0
                def trn(e):
                    r = None
                    for tb in range(4):
                        r = e.transpose(out=psb[bk][:, tb * 128:(tb + 1) * 128], in_=atok[ar][:, tb, :], identity=identb[:, :])
                    return r
                S.op("pe", trn, reads=[Ukey(6 + ar), "identb"], writes=["B%d" % bk])
                mv_, mk_ = mixT_view(j, h)
                S.op("dve", lambda e: e.tensor_copy(out=mv_, in_=psb[bk][:, 0:512]), reads=["B%d" % bk], writes=[mk_])

            return fin

        load_x_tile(2 * (nslot_a - 1))
        slot_pre(nslot_a - 1, ["dve"])
        sgu_hooks = slot_mid(nslot_a - 1)
        for j in range(nslot_a - 1, -1, -1):
            pend = None
            for h in range(4):
                hooks = {}
                if h == 0:
                    for g in range(4):
                        hooks.setdefault(3 + g, []).append(sgu_hooks[g])
                hooks.setdefault(5, []).append(issue_cast)
                if pend is not None:
                    hooks.setdefault(6, []).append(pend)
                pend = attention_head(j, h, hooks)
            for g in range(4):
                mv_, mk_ = mixT_view(j, 4 + g)
                S.op("pool", lambda e, g=g, mv_=mv_: e.tensor_copy(out=mv_, in_=goT[:, g, :]), reads=[("goT", g)], writes=[mk_])
            if j > 0:
                slot_pre(j - 1, ["act"])
            pend()
            if j > 0:
                sgu_hooks = slot_mid(j - 1)
        while deferred_casts:
            issue_cast()
        S.barrier()

        if debug == "mixT":
            for c in range(8):
                mv_, mk_ = mixT_view(0, c)
                S.op("dve", lambda e, mv_=mv_: e.tensor_copy(out=xt[0][:, 0:512], in_=mv_), reads=[mk_, "xt0"], writes=["xt0"])
                S.dma("sp", "dbg", lambda e, c=c: e.dma_start(out=dbg[:, c * 512:(c + 1) * 512], in_=xt[0][:, 0:512]), reads=["xt0"], writes=["dbgd"])

        wrot = {"i": 0}

        def next_w():
            i = wrot["i"] % 6
            wrot["i"] += 1
            return i

        PO = [0, 1]
        PD = [2, 3]
        PG = [4, 5]
        PU = [6, 7]
        bufX = [[xt[tb][:, :] for tb in range(4)], [x1[:, tb, :] for tb in range(4)]]
        bufK = [["xt%d" % tb for tb in range(4)], ["x1_%d" % tb for tb in range(4)]]
        hTs = [(hT, "hT"), (hT, "hT")]
        sgs = [sgbuf2[:, 0:512], sgbuf2[:, 512:1024]]

        def layer_norm_inplace(buf, bkey, Gt, Bt, gkey, bkey2):
            def st_(e):
                e.bn_stats(out=lnst[:, 0:6], in_=buf[:, 0:512])
                return e.bn_stats(out=lnst[:, 6:12], in_=buf[:, 512:1024])
            S.op("dve", st_, reads=[bkey], writes=["lnst"])
            S.op("dve", lambda e: e.bn_aggr(out=lnmv, in_=lnst), reads=["lnst"], writes=["lnmv"])
            S.op("dve", lambda e: e.tensor_scalar(out=lnr[:, 0:1], in0=lnmv[:, 1:2], scalar1=LN_EPS, scalar2=None,
                                                  op0=ALU.add), reads=["lnmv"], writes=["lnr0"])
            S.op("pool", lambda e: e.tensor_tensor(out=lnr[:, 0:1], in0=lnr[:, 0:1], in1=nhalf[:, 0:1], op=ALU.pow),
                 reads=["lnr0"], writes=["lnr0"])
            S.op("dve", lambda e: e.scalar_tensor_tensor(out=lnr[:, 1:2], in0=lnmv[:, 0:1], scalar=-1.0, in1=lnr[:, 0:1],
                                                         op0=ALU.mult, op1=ALU.mult), reads=["lnmv", "lnr0"], writes=["lnr1"])
            S.op("act", lambda e: e.activation(out=buf, in_=buf, func=AF.Identity, bias=lnr[:, 1:2], scale=lnr[:, 0:1]),
                 reads=[bkey, "lnr0", "lnr1"], writes=[bkey])
            S.op("dve", lambda e: e.tensor_tensor(out=buf, in0=buf, in1=Gt, op=ALU.mult), reads=[bkey, gkey], writes=[bkey])
            S.op("dve", lambda e: e.tensor_tensor(out=buf, in0=buf, in1=Bt, op=ALU.add), reads=[bkey, bkey2], writes=[bkey])

        def stage_X(j):
            s_ = j % 2
            for tb in range(4):
                r0 = 2 * j * 512 + tb * 128
                buf, bkey = bufX[s_][tb], bufK[s_][tb]
                S.dma("pool", "p_" + bkey, lambda e, buf=buf, r0=r0: e.dma_start(out=buf, in_=xs[r0:r0 + 128, :]), writes=[bkey])

        def stage_A(j):
            s_ = j % 2
            wo_i = []
            for dh in range(2):
                i = next_w()
                load_w(i, wo_b, 0, 8, dh * 512, 512)
                wo_i.append(i)
            for tb in range(4):
                buf, bkey = bufX[s_][tb], bufK[s_][tb]
                for dh in range(2):
                    bk = next_bank(PO)
                    def mo(e, dh=dh, bk=bk, tb=tb, j=j, wo_i=tuple(wo_i)):
                        r = None
                        for c in range(8):
                            mvw, _ = mixT_view(j, c)
                            r = e.matmul(out=ps[bk][:, :], lhsT=mvw[:, tb * 128:(tb + 1) * 128], rhs=wbuf[wo_i[dh]][:, c, :],
                                         start=(c == 0), stop=(c == 7))
                        return r
                    S.op("pe", mo, reads=[mixT_view(j, c)[1] for c in range(8)] + ["wbuf%d" % wo_i[dh]], writes=["B%d" % bk])
                    S.op("dve", lambda e, dh=dh, bk=bk, buf=buf: e.scalar_tensor_tensor(
                        out=buf[:, dh * 512:(dh + 1) * 512], in0=buf[:, dh * 512:(dh + 1) * 512], scalar=ALPHA, in1=ps[bk][:, :],
                        op0=ALU.mult, op1=ALU.add), reads=[bkey, "B%d" % bk], writes=[bkey])
                layer_norm_inplace(buf, bkey, L1G, L1B, "lnb0", "lnb1")

        def stage_T(j):
            s_ = j % 2
            transpose_mod(bufX[s_], bufK[s_], sc2c, sh2c, PO, ["act", "dve"], dst=hTs[s_][0], dk=hTs[s_][1])

        def stage_GU(j, hooks={}):
            s_ = j % 2
            hsrc, hk = hTs[s_]
            for fg in range(6):
                nf = 4 if fg < 5 else 2
                ig = next_w()
                load_w(ig, wg_b, 0, 8, fg * 512, nf * 128)
                iu = next_w()
                load_w(iu, wu_b, 0, 8, fg * 512, nf * 128)
                for fl in range(nf):
                    fc = fg * 4 + fl
                    bg = next_bank(PG)
                    bu = next_bank(PU)
                    proj_fm(ig, fl, bg, src=hsrc, sk=hk)
                    proj_fm(iu, fl, bu, src=hsrc, sk=hk)
                    sk = ("sg", fc % 2)
                    sgb_ = sgs[fc % 2]
                    S.op("act", lambda e, bg=bg, sgb_=sgb_: e.activation(out=sgb_, in_=ps[bg][:, :], func=AF.Silu),
                         reads=["B%d" % bg], writes=[sk])
                    S.op("dve", lambda e, fc=fc, bu=bu, sgb_=sgb_: e.tensor_tensor(out=actT[:, fc, :], in0=ps[bu][:, :], in1=sgb_,
                                                                                  op=ALU.mult),
                         reads=["B%d" % bu, sk], writes=[("actT", fc)])
                    if fc in hooks:
                        hooks[fc]()

        def stage_DN(j):
            s_ = j % 2
            for dh in range(2):
                for (k0, nk) in ((0, 8), (8, 8), (16, 6)):
                    i = next_w()
                    load_w(i, wd_b, k0, nk, dh * 512, 512)
                    for tb in range(4):
                        bk = dh * 4 + tb
                        def md(e, tb=tb, bk=bk, i=i, k0=k0, nk=nk):
                            r = None
                            for kk in range(nk):
                                fc = k0 + kk
                                r = e.matmul(out=ps[bk][:, :], lhsT=actT[:, fc, tb * 128:(tb + 1) * 128], rhs=wbuf[i][:, kk, :],
                                             start=(fc == 0), stop=(fc == 21))
                            return r
                        S.op("pe", md, reads=[("actT", fc) for fc in range(k0, k0 + nk)] + ["wbuf%d" % i], writes=["B%d" % bk])
                for tb in range(4):
                    buf, bkey = bufX[s_][tb], bufK[s_][tb]
                    bk = dh * 4 + tb
                    S.op("dve", lambda e, dh=dh, bk=bk, buf=buf: e.scalar_tensor_tensor(
                        out=buf[:, dh * 512:(dh + 1) * 512], in0=buf[:, dh * 512:(dh + 1) * 512], scalar=ALPHA, in1=ps[bk][:, :],
                        op0=ALU.mult, op1=ALU.add), reads=[bkey, "B%d" % bk], writes=[bkey])

        def stage_OUT(j, tb):
            s_ = j % 2
            buf, bkey = bufX[s_][tb], bufK[s_][tb]
            layer_norm_inplace(buf, bkey, L2G, L2B, "lnb2", "lnb3")
            r0 = j * 512 + tb * 128
            S.dma("pool", "st%d_%d" % (s_, tb), lambda e: e.dma_start(out=y[r0:r0 + 128, :], in_=buf),
                  reads=[bkey], writes=[("y", j, tb)])
            if j + 2 < nslot_b:
                r1 = 2 * (j + 2) * 512 + tb * 128
                S.dma("pool", "p_" + bkey, lambda e: e.dma_start(out=buf, in_=xs[r1:r1 + 128, :]), writes=[bkey])

        if nslot_b > 0:
            stage_X(0)
            for i, (tile_, src) in enumerate(((L1G, ln1_g), (L1B, ln1_b), (L2G, ln2_g), (L2B, ln2_b))):
                S.dma("pool", "lnb%d" % i, lambda e, tile_=tile_, src=src: e.dma_start(
                    out=tile_, in_=src.rearrange("a b -> (a b)").partition_broadcast(128)), writes=["lnb%d" % i])
            stage_A(0)
            stage_T(0)
        if nslot_b > 1:
            stage_X(1)
        for j in range(nslot_b):
            hooks = {}
            if j > 0:
                for tb in range(4):
                    hooks[2 + 5 * tb] = (lambda j=j, tb=tb: stage_OUT(j - 1, tb))
            stage_GU(j, hooks)
            if j + 1 < nslot_b:
                stage_A(j + 1)
            stage_DN(j)
            if j + 1 < nslot_b:
                stage_T(j + 1)
        for tb in range(4):
            stage_OUT(nslot_b - 1, tb)
        keys = [k for k in S.dma_sems if isinstance(k, str) and k.startswith("st")]
        if debug:
            keys.append("dbg")
        S.final_wait("sp", keys)
        S.final_wait("pool", keys)
        with nc.Block() as block:
            S.emit(block)
    return nc


_NC_CACHE = {}


def _core_layout(c):
    b, p = c // 2, c % 2
    own, partner, vis = [], [], []
    for j in range(NSLOT):
        o = 2 * j + ((j & 1) ^ p)
        pr = 4 * j + 1 - o
        own.append(o)
        partner.append(pr)
        vis.append(pr < o)
    return b, own, partner, vis


def make_in_maps(inp):
    x = np.asarray(inp["x"], dtype=np.float32)
    c = np.asarray(inp["c"], dtype=np.float32)
    sq = lambda k: np.ascontiguousarray(np.asarray(inp[k], dtype=np.float32)[0])
    shared = {
        "w_ada": sq("w_ada"), "b_ada": sq("b_ada").reshape(1, -1), "w_in": sq("w_in"),
        "lam4": np.ascontiguousarray(np.stack([sq("lambda_q1"), sq("lambda_k1"), sq("lambda_q2"), sq("lambda_k2")], 0)),
        "subln_g": sq("subln_g").reshape(1, -1), "sgu_ln_g": sq("sgu_ln_g").reshape(1, -1),
        "sgu_ln_b": sq("sgu_ln_b").reshape(1, -1), "w_spatial": sq("w_spatial"), "b_spatial": sq("b_spatial"),
        "w_out": sq("w_out"), "ln1_g": sq("ln1_g").reshape(1, -1), "ln1_b": sq("ln1_b").reshape(1, -1),
        "w_gate": sq("w_gate"), "w_up": sq("w_up"), "w_down": sq("w_down"),
        "ln2_g": sq("ln2_g").reshape(1, -1), "ln2_b": sq("ln2_b").reshape(1, -1),
    }
    maps = []
    for core in range(8):
        b, own, partner, vis = _core_layout(core)
        xb = x[b].reshape(16, 512, D)
        order = []
        for j in range(NSLOT):
            order += [own[j], partner[j]]
        xs = np.ascontiguousarray(xb[order].reshape(SEQ, D))
        pb = np.zeros((128, 8), np.float32)
        for j in range(NSLOT):
            if not vis[j]:
                pb[:, j] = NEG
        m = dict(shared)
        m["xs"] = xs
        m["ccol"] = np.ascontiguousarray(c[b].reshape(8, 128).T)
        m["pbias"] = pb
        maps.append(m)
    return maps


def gather_out(results):
    out = np.zeros((NB, SEQ, D), np.float32)
    for core in range(8):
        b, own, partner, vis = _core_layout(core)
        yc = np.asarray(results[core]["y"]).reshape(NSLOT, 512, D)
        ob = out[b].reshape(16, 512, D)
        for j in range(NSLOT):
            ob[own[j]] = yc[j]
    return out


def kernel(**inputs):
    if "nc" not in _NC_CACHE:
        _NC_CACHE["nc"] = build_nc()
    nc = _NC_CACHE["nc"]
    maps = make_in_maps(inputs)
    res = run_bass_kernel_spmd(nc, maps, core_ids=list(range(8)))
    return gather_out(res.results)
```
